# Optimizing a Trainium2 kernel written in Bass

```python
import math
import jax, jax.numpy as jnp
from jax import lax
import numpy as np

D_MODEL = 4096
BATCH = 1
SEQ = 8192
DEPTH = 2

N_MIXERS = 2
N_HEADS = 32
HEAD_DIM = D_MODEL // N_HEADS
ATT_WIDTH = N_HEADS * HEAD_DIM
CONV_WIDTH = D_MODEL
CONV_GROUPS = 32
CONV_K = 3
Q_BLOCK = 128
RMS_EPS = 1e-6

kernel_name = "hybrid_stickbreak_shortconv_trunk"


def rms_norm(x, g):
    xf = x.astype(jnp.float32)
    y = xf * lax.rsqrt(jnp.mean(xf * xf, axis=-1, keepdims=True) + RMS_EPS)
    return (y * g.astype(jnp.float32)).astype(x.dtype)


def stick_breaking_attention(q, k, v):
    b, h, s, dh = q.shape
    nb = s // Q_BLOCK
    scale = 1.0 / math.sqrt(dh)
    q_blocks = q.reshape(b, h, nb, Q_BLOCK, dh).transpose(2, 0, 1, 3, 4)
    s_pos = jnp.arange(s)

    def one_block(args):
        blk, qb = args
        t_pos = blk * Q_BLOCK + jnp.arange(Q_BLOCK)
        causal = s_pos[None, :] < t_pos[:, None]
        z = jnp.einsum('bhqd,bhkd->bhqk', qb, k,
                       preferred_element_type=jnp.float32) * scale
        log_fail = jnp.where(causal, jax.nn.log_sigmoid(-z), 0.0)
        suffix = lax.cumsum(log_fail, axis=3, reverse=True) - log_fail
        log_a = jax.nn.log_sigmoid(z) + suffix
        a = jnp.where(causal, jnp.exp(jnp.where(causal, log_a, 0.0)), 0.0)
        return jnp.einsum('bhqk,bhkd->bhqd', a.astype(v.dtype), v)

    out = lax.map(one_block, (jnp.arange(nb), q_blocks))
    return out.transpose(1, 2, 0, 3, 4).reshape(b, h, s, dh)


def attn_branch(xn, w_in, w_out):
    b, s, _ = xn.shape
    proj = xn @ w_in
    q, k, v, g = jnp.split(proj, 4, axis=-1)
    to_heads = lambda t: t.reshape(b, s, N_HEADS, HEAD_DIM).transpose(0, 2, 1, 3)
    o = stick_breaking_attention(to_heads(q), to_heads(k), to_heads(v))
    o = o.transpose(0, 2, 1, 3).reshape(b, s, ATT_WIDTH)
    return (o * jax.nn.silu(g)) @ w_out


def causal_depthwise_conv(u, w):
    c = u.shape[-1]
    return lax.conv_general_dilated(
        u, w.astype(u.dtype)[:, None, :], window_strides=(1,),
        padding=((CONV_K - 1, 0),), dimension_numbers=('NWC', 'WIO', 'NWC'),
        feature_group_count=c)


def conv_branch(xn, w_in, conv_w, w_out):
    proj = xn @ w_in
    gb, gc, u, g = jnp.split(proj, 4, axis=-1)
    y = gb * causal_depthwise_conv(gc * u, conv_w)
    return (y * jax.nn.silu(g)) @ w_out


def setup_inputs(seed: int = 0) -> dict:
    key = jax.random.key(seed)
    ks = jax.random.split(key, 10)
    f32 = jnp.float32
    nrm = lambda k, shape, fan_in: jax.random.normal(k, shape, f32) * (fan_in ** -0.5)
    gain = lambda k: 1.0 + 0.02 * jax.random.normal(k, (D_MODEL,), f32)
    return {
        "x": jax.random.normal(ks[0], (BATCH, SEQ, D_MODEL), f32),
        "norm_attn": gain(ks[1]),
        "w_in_attn": nrm(ks[2], (D_MODEL, 4 * ATT_WIDTH), D_MODEL),
        "w_out_attn": nrm(ks[3], (ATT_WIDTH, D_MODEL), ATT_WIDTH),
        "norm_conv": gain(ks[4]),
        "w_in_conv": nrm(ks[5], (D_MODEL, 4 * CONV_WIDTH), D_MODEL),
        "conv_w": nrm(ks[6], (CONV_K, CONV_WIDTH), CONV_K),
        "w_out_conv": nrm(ks[7], (CONV_WIDTH, D_MODEL), CONV_WIDTH),
        "final_norm": gain(ks[8]),
    }


def reference(x, norm_attn, w_in_attn, w_out_attn, norm_conv, w_in_conv, conv_w,
              w_out_conv, final_norm):
    h = x
    for i in range(DEPTH):
        if i % N_MIXERS == 0:
            h = h + attn_branch(rms_norm(h, norm_attn), w_in_attn, w_out_attn)
        else:
            h = h + conv_branch(rms_norm(h, norm_conv), w_in_conv, conv_w, w_out_conv)
    return rms_norm(h, final_norm)
```

```python
import numpy as np
import concourse.bass as bass
import concourse.mybir as mybir
from concourse.bass_utils import run_bass_kernel_spmd

F32 = mybir.dt.float32
BF16 = mybir.dt.bfloat16
AF = mybir.ActivationFunctionType
ALU = mybir.AluOpType

PE, ACT, DVE, POOL, SP = "tensor", "scalar", "vector", "gpsimd", "sync"
ENGS = (PE, ACT, DVE, POOL, SP)


class Res:
    __slots__ = ("writer", "readers", "name")

    def __init__(self, name=""):
        self.writer = None
        self.readers = {}
        self.name = name


class Op:
    __slots__ = ("eng", "fn", "deps", "is_dma", "sem", "val", "needed", "idx")

    def __init__(self, eng, fn, deps, is_dma):
        self.eng = eng
        self.fn = fn
        self.deps = deps
        self.is_dma = is_dma
        self.sem = None
        self.val = None
        self.needed = False
        self.idx = None


class Prog:
    def __init__(self, nc, same_engine_sync=True, dma_ring=6):
        self.nc = nc
        self.ops = {e: [] for e in ENGS}
        self.same_engine_sync = same_engine_sync
        self.dma_ring = dma_ring
        self.dma_hist = {e: [] for e in ENGS}
        self.n_ops = 0
        self.barrier_deps = []
        self.barrier_seen = set()

    def _deps(self, reads, writes, extra):
        deps = []
        for r in reads:
            if r.writer is not None:
                deps.append(r.writer)
        for w in writes:
            if w.writer is not None:
                deps.append(w.writer)
            deps.extend(w.readers.values())
        deps.extend(extra)
        return deps

    def _bar(self, eng, deps):
        if self.barrier_deps and eng not in self.barrier_seen:
            self.barrier_seen.add(eng)
            deps.extend(self.barrier_deps)
        return deps

    def _commit(self, op, reads, writes):
        for r in reads:
            if op.is_dma:
                r.readers[("dma", id(op))] = op
            else:
                r.readers[op.eng] = op
        for w in writes:
            w.writer = op
            w.readers = {}

    def add(self, eng, fn, reads=(), writes=(), extra=()):
        op = Op(eng, fn, self._bar(eng, self._deps(reads, writes, extra)), False)
        self._commit(op, reads, writes)
        self.ops[eng].append(op)
        self.n_ops += 1
        return op

    def dma(self, eng, fn, reads=(), writes=(), extra=()):
        deps = self._bar(eng, self._deps(reads, writes, extra))
        hist = self.dma_hist[eng]
        if len(hist) >= self.dma_ring:
            deps.append(hist[-self.dma_ring])
        op = Op(eng, fn, deps, True)
        op.idx = len(hist)
        hist.append(op)
        self._commit(op, reads, writes)
        self.ops[eng].append(op)
        self.n_ops += 1
        return op

    def emit(self, final_wait_ops=()):
        nc = self.nc
        for e in ENGS:
            for op in self.ops[e]:
                for d in op.deps:
                    if d.is_dma or d.eng != op.eng or (self.same_engine_sync and op.eng != PE) or op.is_dma:
                        d.needed = True
        for op in final_wait_ops:
            op.needed = True
        import contextlib
        with contextlib.ExitStack() as st:
            eng_sem = {e: st.enter_context(nc.semaphore("p_" + e)) for e in (PE, ACT, DVE, POOL)}
            dma_sems = {e: [st.enter_context(nc.semaphore("d_%s_%d" % (e, i))) for i in range(self.dma_ring)]
                        for e in ENGS if self.dma_hist[e]}
            for e in ENGS:
                cnt = 0
                ring_cnt = [0] * self.dma_ring
                for op in self.ops[e]:
                    if op.is_dma:
                        s = op.idx % self.dma_ring
                        ring_cnt[s] += 16
                        op.sem = dma_sems[e][s]
                        op.val = ring_cnt[s]
                    elif op.needed:
                        cnt += 1
                        op.sem = eng_sem[e]
                        op.val = cnt
            block = st.enter_context(nc.Block())
            for e in ENGS:
                ops = self.ops[e]
                if not ops and not (e == SP and final_wait_ops):
                    continue
                tail = list(final_wait_ops) if e == SP else []

                def body(eng, ops=ops, e=e, tail=tail):
                    waited = {}
                    def do_wait(d):
                        if d.sem is None:
                            return
                        k = id(d.sem)
                        if waited.get(k, 0) >= d.val:
                            return
                        eng.wait_ge(d.sem, d.val)
                        waited[k] = d.val
                    for op in ops:
                        best = {}
                        for d in op.deps:
                            if (not d.is_dma) and d.eng == e and not op.is_dma:
                                if e == PE or not self.same_engine_sync:
                                    continue
                            if d.sem is None:
                                continue
                            k = id(d.sem)
                            if k not in best or best[k].val < d.val:
                                best[k] = d
                        for d in best.values():
                            do_wait(d)
                        ins = op.fn(eng)
                        if op.is_dma:
                            ins.then_inc(op.sem, 16)
                        elif op.needed:
                            ins.then_inc(op.sem, 1)
                    for d in tail:
                        do_wait(d)
                getattr(block, e)(body)


def prog_barrier(p):
    deps = []
    for e in ENGS:
        ops = p.ops[e]
        comp = [o for o in ops if not o.is_dma]
        if comp:
            deps.append(comp[-1])
        deps.extend(p.dma_hist[e][-p.dma_ring:])
    p.barrier_deps = deps
    p.barrier_seen = set()


class Arena:
    def __init__(self, t, nbytes):
        self.t = t
        self.n = nbytes
        self.top = 0

    def alloc(self, nbytes, align=64):
        off = (self.top + align - 1) // align * align
        assert off + nbytes <= self.n, ("arena overflow", off, nbytes, self.n)
        self.top = off + nbytes
        return off

    def view(self, off, shape, dt, parts=128):
        esz = 2 if dt == BF16 else 4
        n = int(np.prod(shape))
        ap = self.t[0:parts, off // 2: off // 2 + n * esz // 2]
        if dt != BF16:
            ap = ap.bitcast(dt)
        if len(shape) == 2:
            ap = ap.rearrange("p (a b) -> p a b", a=shape[0])
        elif len(shape) == 3:
            ap = ap.rearrange("p (a b c) -> p a b c", a=shape[0], b=shape[1])
        return ap


import contextlib

RMS_EPS = 1e-6
KIB = 1024


def _consts(nc, st, p):
    ident = st.enter_context(nc.sbuf_tensor("ident", [128, 128], BF16))
    U = st.enter_context(nc.sbuf_tensor("Utri", [128, 128], BF16))
    NU = st.enter_context(nc.sbuf_tensor("NUtri", [128, 128], BF16))
    Rc = Res("consts")
    p.add(POOL, lambda q: q.memset(ident[:], 1.0), writes=[Rc])
    p.add(POOL, lambda q: q.affine_select(out=ident[:], in_=ident[:], pattern=[[-1, 128]], compare_op=ALU.is_equal,
                                          fill=0.0, base=0, channel_multiplier=1), writes=[Rc])
    p.add(POOL, lambda q: q.memset(U[:], 1.0), writes=[Rc])
    p.add(POOL, lambda q: q.affine_select(out=U[:], in_=U[:], pattern=[[-1, 128]], compare_op=ALU.is_ge,
                                          fill=0.0, base=0, channel_multiplier=1), writes=[Rc])
    p.add(POOL, lambda q: q.memset(NU[:], 1.0), writes=[Rc])
    p.add(POOL, lambda q: q.affine_select(out=NU[:], in_=NU[:], pattern=[[1, 128]], compare_op=ALU.is_gt,
                                          fill=0.0, base=0, channel_multiplier=-1), writes=[Rc])
    return ident, U, NU, Rc


def _prep_tile(p, nc, src_ap, npart, D, x32, Rx32, junk, Rjunk, ss, rs, Rst, gainb, Rgain, xs, Rxs,
               ident, Rc, banks_bf, RB, bank_ids, dst_fn, Rdst, ev_flip, phase=0):
    if phase in (0, 1):
        _prep_tile_a(p, nc, src_ap, npart, D, x32, Rx32, junk, Rjunk, ss, rs, Rst, gainb, Rgain, xs, Rxs)
    if phase in (0, 2):
        _prep_tile_b(p, npart, D, xs, Rxs, ident, Rc, banks_bf, RB, bank_ids, dst_fn, Rdst, ev_flip)


def _prep_tile_a(p, nc, src_ap, npart, D, x32, Rx32, junk, Rjunk, ss, rs, Rst, gainb, Rgain, xs, Rxs):
    FC = D // 128
    p.dma(SP, lambda q: q.dma_start(out=x32[0:npart, :], in_=src_ap), writes=[Rx32])
    p.add(ACT, lambda q: q.activation(out=junk[0:npart, :], in_=x32[0:npart, :], func=AF.Square, accum_out=ss[0:npart, :]),
          reads=[Rx32], writes=[Rjunk, Rst])
    p.add(ACT, lambda q: q.activation(out=rs[0:npart, :], in_=ss[0:npart, :], func=AF.Ln, scale=1.0 / D, bias=RMS_EPS),
          reads=[Rst], writes=[Rst])
    p.add(ACT, lambda q: q.activation(out=rs[0:npart, :], in_=rs[0:npart, :], func=AF.Exp, scale=-0.5), reads=[Rst], writes=[Rst])
    p.add(DVE, lambda q: q.scalar_tensor_tensor(out=xs[0:npart, :], in0=x32[0:npart, :], scalar=rs[0:npart, 0:1],
                                                in1=gainb[0:npart, :], op0=ALU.mult, op1=ALU.mult),
          reads=[Rx32, Rst, Rgain], writes=[Rxs])


def _prep_tile_b(p, npart, D, xs, Rxs, ident, Rc, banks_bf, RB, bank_ids, dst_fn, Rdst, ev_flip):
    FC = D // 128
    nb = (FC + 7) // 8
    for bi in range(nb):
        b = bank_ids[bi]
        c0 = bi * 8
        n = min(8, FC - c0)
        for c in range(c0, c0 + n):
            p.add(PE, lambda q, c=c, b=b, c0=c0: q.transpose(out=banks_bf[b][:, c - c0, 0:npart], in_=xs[0:npart, c * 128:(c + 1) * 128],
                                                         identity=ident[0:npart, 0:npart]),
                  reads=[Rxs, Rc], writes=[RB[b]])
        eng = ACT if (bi + ev_flip) % 2 == 0 else DVE
        rd = Rdst[bi] if isinstance(Rdst, (list, tuple)) else Rdst
        if eng == ACT:
            p.add(ACT, lambda q, b=b, c0=c0, n=n: q.copy(out=dst_fn(c0, n), in_=banks_bf[b][:, 0:n, 0:npart]),
                  reads=[RB[b]], writes=[rd])
        else:
            p.add(DVE, lambda q, b=b, c0=c0, n=n: q.tensor_copy(out=dst_fn(c0, n), in_=banks_bf[b][:, 0:n, 0:npart]),
                  reads=[RB[b]], writes=[rd])


def build_A(S, FC, NH, NCV=0):
    D = FC * 128
    NTG = S // 512
    NKB = S // 128
    scale = 1.0 / (128 ** 0.5)
    nc = bass.Bass("TRN2", target_bir_lowering=False)
    x = nc.dram_tensor("x", [S, D], F32, kind="ExternalInput").ap()
    gain_d = nc.dram_tensor("gain_b", [128, D], F32, kind="ExternalInput").ap()
    w = nc.dram_tensor("w", [NH, 128, FC * 512], F32, kind="ExternalInput").ap()
    og = nc.dram_tensor("og", [NH, 128, S], BF16, kind="ExternalOutput").ap()
    scr = nc.dram_tensor("xT_scr", [NTG, 128, FC * 512], BF16, kind="Internal").ap()
    if NCV:
        wcv_in = nc.dram_tensor("wcv_in", [NCV * 128, 2048], F32, kind="ExternalInput").ap()
        wcv_out = nc.dram_tensor("wcv_out", [NCV * 128, 2048], BF16, kind="ExternalOutput").ap()
    cv_next = [0]

    with contextlib.ExitStack() as st:
        p = Prog(nc)
        ident, U, NU, Rc = _consts(nc, st, p)
        small = st.enter_context(nc.sbuf_tensor("small", [128, 16], F32))
        NX32 = 4
        szP = D * 4 + NX32 * D * 4 + 2 * D * 2 + 2 * FC * 1024
        szH = 4 * S * 2 + FC * 1024 + 2 * FC * 1024
        ring_sz = 5 * 2048 + 4 * 1024 + 3 * 2048 + 3 * 1024 + 2 * 1024 + 2 * 2048 + 2 * 1024
        tot = max(szP, szH) + ring_sz + 1024
        art = st.enter_context(nc.sbuf_tensor("arena", [128, tot // 2], BF16))
        ar = Arena(art, tot)
        base = ar.alloc(max(szP, szH))
        o = base
        gainb = ar.view(o, [D], F32); o += D * 4
        x32 = []
        for i in range(NX32):
            x32.append(ar.view(o, [D], F32)); o += D * 4
        xs = []
        for i in range(2):
            xs.append(ar.view(o, [D], BF16)); o += D * 2
        xTg, xTg_flat = [], []
        for i in range(2):
            xTg.append(ar.view(o, [FC, 512], BF16)); xTg_flat.append(ar.view(o, [FC * 512], BF16)); o += FC * 1024
        o = base
        qT = ar.view(o, [S], BF16); o += S * 2
        sgT = ar.view(o, [S], BF16); o += S * 2
        kT = ar.view(o, [S], BF16); o += S * 2
        vsb = ar.view(o, [NKB, 128], BF16); o += S * 2
        Wh = ar.view(o, [FC, 512], BF16); Wh_flat = ar.view(o, [FC * 512], BF16); o += FC * 1024
        xin, xin_flat = [], []
        for i in range(2):
            xin.append(ar.view(o, [FC, 512], BF16)); xin_flat.append(ar.view(o, [FC * 512], BF16)); o += FC * 1024
        def ring(n, dt):
            esz = 4 if dt == F32 else 2
            return [ar.view(ar.alloc(512 * esz), [512], dt) for _ in range(n)]
        e_r = ring(5, F32); L_r = ring(4, BF16); w_r = ring(3, F32); a_r = ring(3, BF16)
        vst = ring(2, BF16); gtm = ring(2, F32); ogs = ring(2, BF16)
        Re = [Res() for _ in e_r]; RL = [Res() for _ in L_r]; Rw = [Res() for _ in w_r]; Ra = [Res() for _ in a_r]
        Rvst = [Res() for _ in vst]; Rgtm = [Res() for _ in gtm]; Rogs = [Res() for _ in ogs]
        banks = [st.enter_context(nc.psum_tensor("bank%d" % i, [128, 512], F32)) for i in range(8)]
        banks_bf = [b.bitcast(BF16)[:, :].rearrange("p (a b) -> p a b", a=8) for b in banks]
        RB = [Res("bank%d" % i) for i in range(8)]

        Rgain = Res(); Rx32 = [Res() for _ in range(NX32)]; Rxs = [Res(), Res()]
        Rst = [Res() for _ in range(4)]
        RxTg = [[[Res() for _ in range(4)] for _ in range(4)] for _ in range(2)]
        Rscr = [Res() for _ in range(NTG)]
        p.dma(SP, lambda q: q.dma_start(out=gainb, in_=gain_d), writes=[Rgain])
        nbk = (FC + 7) // 8
        def stageP(t, phase):
            tg, tt = t // 4, t % 4
            bsel = tg % 2
            xb = t % 2
            x4 = t % NX32
            sj = t % 4
            bank_ids = [((t % 2) * 4 + i) % 8 for i in range(nbk)]
            _prep_tile(p, nc, x[t * 128:(t + 1) * 128, :], 128, D, x32[x4], Rx32[x4], xs[xb], Rxs[xb],
                       small[:, 2 * sj:2 * sj + 1], small[:, 2 * sj + 1:2 * sj + 2], Rst[sj], gainb, Rgain,
                       xs[xb], Rxs[xb], ident, Rc, banks_bf, RB, bank_ids,
                       lambda c0, n, bsel=bsel, tt=tt: xTg[bsel][:, c0:c0 + n, tt * 128:(tt + 1) * 128],
                       RxTg[bsel][tt], t, phase=phase)
            if phase == 2 and tt == 3:
                p.dma(POOL, lambda q, tg=tg, bsel=bsel: q.dma_start(out=scr[tg], in_=xTg_flat[bsel]),
                      reads=[r for rr in RxTg[bsel] for r in rr], writes=[Rscr[tg]])
        NT_ = NTG * 4
        stageP(0, 1)
        for t in range(NT_):
            if t + 1 < NT_:
                stageP(t + 1, 1)
            stageP(t, 2)
        prog_barrier(p)

        RW = Res(); Rxin = [Res(), Res()]
        RqT = [Res() for _ in range(NTG)]; RsgT = [Res() for _ in range(NTG)]
        RkT = [Res() for _ in range(NTG)]; Rv = [Res() for _ in range(NTG)]
        out_ops = []
        bank_rr = [0]
        WCH = 2048
        IPB = [5, 6]
        TB = 7
        ZB = [0, 1]; CB = 2; OB = [3, 4]
        CHUNK = 4
        NCH = 4 * (FC // CHUNK) + 1

        def load_W(h):
            for k in range(FC * 512 // WCH):
                p.dma(POOL, lambda q, h=h, k=k: q.dma_start(out=Wh_flat[:, k * WCH:(k + 1) * WCH], in_=w[h][:, k * WCH:(k + 1) * WCH]),
                      writes=[RW])

        def load_xin(tg):
            xb = tg % 2
            p.dma(SP, lambda q, tg=tg, xb=xb: q.dma_start(out=xin_flat[xb], in_=scr[tg]), reads=[Rscr[tg]], writes=[Rxin[xb]])

        def unit_gen(h, tg):
            xb = tg % 2
            tsl = slice(tg * 512, (tg + 1) * 512)
            for j in (1, 2, 0, 3):
                b = IPB[bank_rr[0] % 2]
                bank_rr[0] += 1
                for c in range(FC):
                    p.add(PE, lambda q, b=b, c=c, j=j, xb=xb: q.matmul(banks[b][:, :], lhsT=Wh[:, c, j * 128:(j + 1) * 128], rhs=xin[xb][:, c, :],
                                                                     start=(c == 0), stop=(c == FC - 1)),
                          reads=[RW, Rxin[xb]], writes=[RB[b]])
                    if c % CHUNK == CHUNK - 1 and c != FC - 1:
                        yield 1
                if j == 0:
                    p.add(DVE, lambda q, b=b, tsl=tsl: q.tensor_copy(out=qT[:, tsl], in_=banks[b][:, :]), reads=[RB[b]], writes=[RqT[tg]])
                elif j == 1:
                    p.add(DVE, lambda q, b=b, tsl=tsl: q.tensor_copy(out=kT[:, tsl], in_=banks[b][:, :]), reads=[RB[b]], writes=[RkT[tg]])
                elif j == 2:
                    r = tg % 2
                    p.add(DVE, lambda q, b=b, r=r: q.tensor_copy(out=vst[r], in_=banks[b][:, :]), reads=[RB[b]], writes=[Rvst[r]])
                    for i in range(4):
                        p.add(PE, lambda q, r=r, i=i: q.transpose(out=banks_bf[TB][:, i, :], in_=vst[r][:, i * 128:(i + 1) * 128], identity=ident[:, :]),
                              reads=[Rvst[r], Rc], writes=[RB[TB]])
                    p.add(ACT, lambda q, tg=tg: q.copy(out=vsb[:, tg * 4:(tg + 1) * 4, :], in_=banks_bf[TB][:, 0:4, :]),
                          reads=[RB[TB]], writes=[Rv[tg]])
                else:
                    r = tg % 2
                    p.add(ACT, lambda q, b=b, r=r: q.activation(out=gtm[r], in_=banks[b][:, :], func=AF.Exp, scale=-1.0),
                          reads=[RB[b]], writes=[Rgtm[r]])
                    p.add(DVE, lambda q, r=r: q.tensor_scalar(out=gtm[r], in0=gtm[r], scalar1=1.0, scalar2=None, op0=ALU.add),
                          reads=[Rgtm[r]], writes=[Rgtm[r]])
                    p.add(DVE, lambda q, r=r: q.reciprocal(out=gtm[r], in_=gtm[r]), reads=[Rgtm[r]], writes=[Rgtm[r]])
                    p.add(DVE, lambda q, b=b, r=r, tsl=tsl: q.tensor_tensor(out=sgT[:, tsl], in0=banks[b][:, :], in1=gtm[r], op=ALU.mult),
                          reads=[RB[b], Rgtm[r]], writes=[RsgT[tg]])
                yield 1
            for _ in range(2):
                if cv_next[0] < NCV:
                    k = cv_next[0]
                    cv_next[0] += 1
                    out_ops.append(p.dma(POOL, lambda q, k=k: q.dma_start(out=wcv_out[k * 128:(k + 1) * 128, :], in_=wcv_in[k * 128:(k + 1) * 128, :])))

        def attention(h):
            st_ = {"done": 0, "gen": None, "left": 0}

            def start_unit():
                tg = st_["done"]
                st_["gen"] = unit_gen(h, tg)
                st_["left"] = NCH

            def advance(n):
                for _ in range(n):
                    if st_["gen"] is None:
                        if st_["done"] >= NTG:
                            return
                        start_unit()
                    if next(st_["gen"], None) is None:
                        st_["gen"] = None
                        st_["done"] += 1
                        nxt = st_["done"] + 1
                        if nxt < NTG:
                            load_xin(nxt)
                        return
                    st_["left"] -= 1

            def pump_until(n_units):
                while st_["done"] < min(n_units, NTG):
                    advance(1)

            load_xin(0)
            if NTG > 1:
                load_xin(1)
            pump_until(1)

            steps = []
            first_step = {}
            for T in range(NTG):
                n = 4 * T + 4
                first_step[T] = len(steps)
                for i in range(n):
                    steps.append((T, i, 4 * T + 3 - i, i == n - 1))
            N = len(steps)

            def valid(j):
                return 0 <= j < N
            def c0_of(i):
                return (3 - i) * 128 if i < 4 else 0
            for j in range(-3, N + 2):
                if valid(j - 1):
                    T, i, kb, last = steps[j - 1]
                    wr = (j - 1) % 3; c0 = c0_of(i)
                    p.add(ACT, lambda q, wr=wr, c0=c0: q.activation(out=w_r[wr][:, c0:512], in_=banks[CB][:, c0:512], func=AF.Exp, scale=-1.0),
                          reads=[RB[CB]], writes=[Rw[wr]])
                if valid(j + 2):
                    T, i, kb, last = steps[j + 2]
                    zb = ZB[(j + 2) % 2]; er = (j + 2) % 5; c0 = c0_of(i)
                    p.add(ACT, lambda q, zb=zb, er=er, c0=c0: q.activation(out=e_r[er][:, c0:512], in_=banks[zb][:, c0:512], func=AF.Exp, scale=scale),
                          reads=[RB[zb]], writes=[Re[er]])
                    if i < 4:
                        p.add(POOL, lambda q, er=er, c0=c0: q.affine_select(out=e_r[er][:, c0:c0 + 128], in_=e_r[er][:, c0:c0 + 128], pattern=[[1, 128]],
                                                                        compare_op=ALU.is_gt, fill=0.0, base=0, channel_multiplier=-1),
                              reads=[Re[er]], writes=[Re[er]])
                if valid(j + 1):
                    T, i, kb, last = steps[j + 1]
                    er = (j + 1) % 5; lr = (j + 1) % 4; c0 = c0_of(i)
                    p.add(ACT, lambda q, er=er, lr=lr, c0=c0: q.activation(out=L_r[lr][:, c0:512], in_=e_r[er][:, c0:512], func=AF.Ln, bias=1.0),
                          reads=[Re[er]], writes=[RL[lr]])
                if valid(j):
                    T, i, kb, last = steps[j]
                    if st_["done"] < NTG and st_["done"] <= T + 1:
                        iters_left = max(1, (4 * T + 4) - i - 3)
                        if st_["gen"] is None:
                            start_unit()
                        advance(-(-st_["left"] // iters_left))
                if valid(j):
                    T, i, kb, last = steps[j]
                    lr = j % 4; lp = (j - 1) % 4; c0 = c0_of(i)
                    if i > 0:
                        pc0 = c0_of(i - 1)
                        p.add(PE, lambda q, lp=lp, pc0=pc0: q.matmul(banks[CB][:, pc0:512], lhsT=NU[:, :], rhs=L_r[lp][:, pc0:512], start=False, stop=False, skip_group_check=True),
                              reads=[Rc, RL[lp]], writes=[RB[CB]])
                    p.add(PE, lambda q, lr=lr, i=i, c0=c0: q.matmul(banks[CB][:, c0:512], lhsT=U[:, :], rhs=L_r[lr][:, c0:512], start=(i == 0), stop=True, skip_group_check=(i > 0)),
                          reads=[Rc, RL[lr]], writes=[RB[CB]])
                if valid(j + 3):
                    T, i, kb, last = steps[j + 3]
                    if i == 0:
                        pump_until(T + 1)
                    zb = ZB[(j + 3) % 2]; c0 = c0_of(i)
                    p.add(PE, lambda q, zb=zb, kb=kb, T=T, c0=c0: q.matmul(banks[zb][:, c0:512], lhsT=kT[:, kb * 128:(kb + 1) * 128], rhs=qT[:, T * 512 + c0:(T + 1) * 512],
                                                                      start=True, stop=True),
                          reads=[RkT[kb // 4], RqT[T]], writes=[RB[zb]])
                if valid(j - 1):
                    T, i, kb, last = steps[j - 1]
                    er = (j - 1) % 5; wr = (j - 1) % 3; ai = (j - 1) % 3; c0 = c0_of(i)
                    ob = OB[T % 2]
                    if c0 > 0:
                        p.add(POOL, lambda q, ai=ai, c0=c0: q.memset(a_r[ai][:, 0:c0], 0.0), writes=[Ra[ai]])
                    p.add(DVE, lambda q, er=er, wr=wr, ai=ai, c0=c0: q.tensor_tensor(out=a_r[ai][:, c0:512], in0=e_r[er][:, c0:512], in1=w_r[wr][:, c0:512], op=ALU.mult),
                          reads=[Re[er], Rw[wr]], writes=[Ra[ai]])
                    p.add(PE, lambda q, ob=ob, kb=kb, ai=ai, i=i, last=last: q.matmul(banks[ob][:, :], lhsT=vsb[:, kb, :], rhs=a_r[ai], start=(i == 0), stop=last),
                          reads=[Rv[kb // 4], Ra[ai]], writes=[RB[ob]])
                    if last:
                        sr = T % 2
                        p.add(DVE, lambda q, ob=ob, sr=sr, T=T: q.tensor_tensor(out=ogs[sr], in0=banks[ob][:, :], in1=sgT[:, T * 512:(T + 1) * 512], op=ALU.mult),
                              reads=[RB[ob], RsgT[T]], writes=[Rogs[sr]])
                        out_ops.append(p.dma(POOL, lambda q, h=h, sr=sr, T=T: q.dma_start(out=og[h][:, T * 512:(T + 1) * 512], in_=ogs[sr]),
                                             reads=[Rogs[sr]]))
                    if T == NTG - 1 and i == 0 and h + 1 < NH:
                        load_W(h + 1)
            pump_until(NTG)

        load_W(0)
        for h in range(NH):
            attention(h)

        p.emit(final_wait_ops=out_ops[-6:])
    return nc


def build_B(NT, FC):
    D = FC * 128
    CG = D // 512
    NTT = NT // 128
    NG = NT // 512
    NH4 = NTT // 4
    AX = mybir.AxisListType.X
    nc = bass.Bass("TRN2", target_bir_lowering=False)
    xh = nc.dram_tensor("xh", [NT + 2, D], F32, kind="ExternalInput").ap()
    ogT_d = nc.dram_tensor("ogT", [128, FC * (NT + 2)], BF16, kind="ExternalInput").ap()
    wo = nc.dram_tensor("wo", [CG, 128, FC * 512], BF16, kind="ExternalInput").ap()
    gconv = nc.dram_tensor("gconv_b", [128, D], F32, kind="ExternalInput").ap()
    wc = nc.dram_tensor("wc", [FC, 4, 128, FC * 128], BF16, kind="ExternalInput").ap()
    convw_d = nc.dram_tensor("convw", [128, FC * 3], F32, kind="ExternalInput").ap()
    wco = nc.dram_tensor("wco", [CG, 128, FC * 512], BF16, kind="ExternalInput").ap()
    gfin = nc.dram_tensor("gfin_b", [128, D], F32, kind="ExternalInput").ap()
    out = nc.dram_tensor("out", [NT, D], F32, kind="ExternalOutput").ap()
    h1s = nc.dram_tensor("h1s", [NT + 2, D], F32, kind="Internal").ap()

    with contextlib.ExitStack() as st:
        p = Prog(nc)
        ident, U, NU, Rc = _consts(nc, st, p)
        small = st.enter_context(nc.sbuf_tensor("small", [128, 16], F32))
        convw = st.enter_context(nc.sbuf_tensor("convw_sb", [128, FC * 3], F32))
        cuh = st.enter_context(nc.sbuf_tensor("cuh", [128, FC, 2], F32))
        uh = st.enter_context(nc.sbuf_tensor("uh", [128, 2], F32))
        ss2 = st.enter_context(nc.sbuf_tensor("ss2", [128, NTT * CG], F32))
        xn2Th = st.enter_context(nc.sbuf_tensor("xn2Th", [128, FC, 2], BF16))
        szR1 = max(FC * (NT + 2) * 2, 2 * D * 4 + D * 2 + D * 2 + D * 4, FC * NT * 2)
        szR1 = (szR1 + 63) // 64 * 64
        szR3 = max(FC * NT * 2, 3 * D * 4)
        WCH = 8192
        NW = 5
        szR2 = 3 * 8192
        tot = szR1 + szR3 + NW * WCH + szR2 + 4 * 2048 + 1024
        art = st.enter_context(nc.sbuf_tensor("arena", [128, tot // 2], BF16))
        ar = Arena(art, tot)
        r1 = ar.alloc(szR1); r3 = ar.alloc(szR3); wr = ar.alloc(NW * WCH); r2 = ar.alloc(szR2); sg_off = ar.alloc(4 * 2048)
        ogT = ar.view(r1, [FC, NT + 2], BF16); ogT_flat = ar.view(r1, [FC * (NT + 2)], BF16)
        o = r1
        x32 = []
        for i in range(2):
            x32.append(ar.view(o, [D], F32)); o += D * 4
        junk = ar.view(o, [D], BF16); o += D * 2
        xs1 = ar.view(o, [D], BF16); o += D * 2
        gainb = ar.view(o, [D], F32); o += D * 4
        yT = ar.view(r1, [FC, NT], BF16)
        xn2T = ar.view(r3, [FC, NT], BF16)
        h2b = [ar.view(r3 + i * D * 4, [D], F32) for i in range(2)]
        gfinb = ar.view(r3 + 2 * D * 4, [D], F32)
        Wfl = [ar.view(wr + i * WCH, [WCH // 2], BF16) for i in range(NW)]
        W512 = [ar.view(wr + i * WCH, [8, 512], BF16) for i in range(NW)]
        W128 = [ar.view(wr + i * WCH, [FC, 128], BF16) for i in range(NW)] if FC * 128 * 2 <= WCH else None
        RWr = [Res() for _ in range(NW)]
        wcnt = [0]
        colt = [ar.view(r2 + i * 8192, [4, 512], F32) for i in range(3)]
        Rcolt = [Res() for _ in range(3)]
        gt = [ar.view(r2 + i * 2048, [512], F32) for i in range(10)]
        cu_ext = [ar.view(r2 + 0, [1024], F32), ar.view(r2 + 4096, [1024], F32)]
        u_sb = [gt[4], gt[5]]; tcv = [gt[6], gt[7]]; sgb = [gt[8], gt[9]]; t2b = [gt[10 - 10 + 0 + 0] if False else ar.view(r2 + 20480, [512], F32), ar.view(r2 + 22528, [512], F32)]
        Rcu = [Res(), Res()]; Ru = [Res(), Res()]; Rt = [Res(), Res()]; Rsg = [Res(), Res()]; Rt2 = [Res(), Res()]
        stg = [ar.view(sg_off + i * 2048, [512], F32) for i in range(4)]
        Rstg = [Res() for _ in range(4)]
        scnt = [0]
        banks = [st.enter_context(nc.psum_tensor("bank%d" % i, [128, 512], F32)) for i in range(8)]
        banks_bf = [b.bitcast(BF16)[:, :].rearrange("p (a b) -> p a b", a=8) for b in banks]
        RB = [Res("bank%d" % i) for i in range(8)]
        bcnt = [0]

        def next_bank(n=7):
            b = bcnt[0] % n
            bcnt[0] += 1
            return b

        def load_w512(src_rows):
            s = wcnt[0] % NW
            wcnt[0] += 1
            p.dma(SP, lambda q, s=s: q.dma_start(out=Wfl[s], in_=src_rows), writes=[RWr[s]])
            return s

        RogT = Res()
        nsp = 4
        tot_og = FC * (NT + 2)
        step_og = (tot_og + nsp - 1) // nsp
        for k in range(nsp):
            a0, a1 = k * step_og, min(tot_og, (k + 1) * step_og)
            p.dma(SP, lambda q, a0=a0, a1=a1: q.dma_start(out=ogT_flat[:, a0:a1], in_=ogT_d[:, a0:a1]), writes=[RogT])
        Rh1 = Res()

        def proj_pass(w_dram, lhs_fn, lhs_res, add_src_fn, dst_fn, halo, sumsq):
            colcnt = 0
            for cg in range(CG):
                slots = [load_w512(w_dram[cg][:, k * 4096:(k + 1) * 4096]) for k in range(FC // 8)]
                csl = slice(cg * 512, (cg + 1) * 512)
                cts = []
                for g in range(NH4):
                    ci = colcnt % 3
                    colcnt += 1
                    src = add_src_fn(g, csl)
                    p.dma(SP, lambda q, ci=ci, src=src: q.dma_start(out=colt[ci], in_=src), writes=[Rcolt[ci]])
                    cts.append(ci)
                for tt in range(NTT):
                    b = next_bank(7)
                    for fc in range(FC):
                        s = slots[fc // 8]
                        p.add(PE, lambda q, b=b, fc=fc, s=s, tt=tt: q.matmul(banks[b][:, :], lhsT=lhs_fn(fc, tt), rhs=W512[s][:, fc % 8, :],
                                                                           start=(fc == 0), stop=(fc == FC - 1)),
                              reads=[lhs_res, RWr[s]], writes=[RB[b]])
                    si = scnt[0] % 4
                    scnt[0] += 1
                    ci = cts[tt // 4]
                    p.add(DVE, lambda q, b=b, si=si, ci=ci, tt=tt: q.tensor_tensor(out=stg[si], in0=banks[b][:, :], in1=colt[ci][:, tt % 4, :], op=ALU.add),
                          reads=[RB[b], Rcolt[ci]], writes=[Rstg[si]])
                    if sumsq:
                        col = tt * CG + cg
                        p.add(ACT, lambda q, si=si, col=col: q.activation(out=junk2, in_=stg[si], func=AF.Square, accum_out=ss2[:, col:col + 1]),
                              reads=[Rstg[si]], writes=[Rjunk2])
                    p.dma(POOL, lambda q, si=si, tt=tt, csl=csl: q.dma_start(out=dst_fn(tt, csl), in_=stg[si]), reads=[Rstg[si]], writes=[Rh1])
                if halo:
                    b = 7
                    for fc in range(FC):
                        s = slots[fc // 8]
                        p.add(PE, lambda q, fc=fc, s=s: q.matmul(banks[b][0:2, :], lhsT=ogT[:, fc, 0:2], rhs=W512[s][:, fc % 8, :],
                                                               start=(fc == 0), stop=(fc == FC - 1)),
                              reads=[lhs_res, RWr[s]], writes=[RB[b]])
                    si = scnt[0] % 4
                    scnt[0] += 1
                    ci = colcnt % 3
                    colcnt += 1
                    p.dma(SP, lambda q, ci=ci, csl=csl: q.dma_start(out=colt[ci][0:2, 0, :], in_=xh[0:2, csl]), writes=[Rcolt[ci]])
                    p.add(DVE, lambda q, si=si, ci=ci: q.tensor_tensor(out=stg[si][0:2, :], in0=banks[b][0:2, :], in1=colt[ci][0:2, 0, :], op=ALU.add),
                          reads=[RB[b], Rcolt[ci]], writes=[Rstg[si]])
                    p.dma(POOL, lambda q, si=si, csl=csl: q.dma_start(out=h1s[0:2, csl], in_=stg[si][0:2, :]), reads=[Rstg[si]], writes=[Rh1])

        junk2 = ar.view(sg_off, [512], F32)
        Rjunk2 = Res()
        proj_pass(wo, lambda fc, tt: ogT[:, fc, 2 + tt * 128: 2 + (tt + 1) * 128], RogT,
                  lambda g, csl: xh[2 + g * 512: 2 + (g + 1) * 512, csl].rearrange("(t p) c -> p t c", p=128),
                  lambda tt, csl: h1s[2 + tt * 128: 2 + (tt + 1) * 128, csl], True, False)
        prog_barrier(p)

        Rgain = Res(); Rx32 = [Res(), Res()]; Rjunk = Res(); Rxs = Res(); Rst = [Res() for _ in range(4)]
        Rxn = [[Res() for _ in range(4)] for _ in range(NTT)]; Rxnh = Res()
        p.dma(SP, lambda q: q.dma_start(out=gainb, in_=gconv), writes=[Rgain])
        nbk = (FC + 7) // 8
        for t in range(NTT + 1):
            xb = t % 2; sj = t % 4
            bank_ids = [((t % 2) * 4 + i) % 8 for i in range(nbk)]
            if t < NTT:
                _prep_tile(p, nc, h1s[2 + t * 128: 2 + (t + 1) * 128, :], 128, D, x32[xb], Rx32[xb], junk, Rjunk,
                           small[:, 2 * sj:2 * sj + 1], small[:, 2 * sj + 1:2 * sj + 2], Rst[sj], gainb, Rgain,
                           xs1, Rxs, ident, Rc, banks_bf, RB, bank_ids,
                           lambda c0, n, t=t: xn2T[:, c0:c0 + n, t * 128:(t + 1) * 128], Rxn[t], t)
            else:
                _prep_tile(p, nc, h1s[0:2, :], 2, D, x32[xb], Rx32[xb], junk, Rjunk,
                           small[:, 2 * sj:2 * sj + 1], small[:, 2 * sj + 1:2 * sj + 2], Rst[sj], gainb, Rgain,
                           xs1, Rxs, ident, Rc, banks_bf, RB, bank_ids,
                           lambda c0, n: xn2Th[:, c0:c0 + n, :], Rxnh, t)
        prog_barrier(p)

        Rcw = Res(); Rcuh = Res(); Ruh = Res()
        RyT = [Res() for _ in range(NG)]
        p.dma(SP, lambda q: q.dma_start(out=convw[:, :], in_=convw_d), writes=[Rcw])
        gcnt = 0
        for cb in range(FC):
            wsl = []
            for part in range(4):
                s = wcnt[0] % NW
                wcnt[0] += 1
                nel = FC * 128
                p.dma(SP, lambda q, s=s, cb=cb, part=part, nel=nel: q.dma_start(out=Wfl[s][:, 0:nel], in_=wc[cb, part]), writes=[RWr[s]])
                wsl.append(s)
            hb = banks[7][:, 0:4].rearrange("p (a b) -> p a b", a=2)
            for pi, part in enumerate((1, 2)):
                s = wsl[part]
                for fc in range(FC):
                    p.add(PE, lambda q, pi=pi, s=s, fc=fc: q.matmul(hb[:, pi, :], lhsT=W128[s][:, fc, :], rhs=xn2Th[:, fc, :], start=(fc == 0), stop=(fc == FC - 1)),
                          reads=[RWr[s], Rxnh], writes=[RB[7]])
            p.add(ACT, lambda q: q.copy(out=uh[:, :], in_=hb[:, 1, :]), reads=[RB[7]], writes=[Ruh])
            p.add(DVE, lambda q, cb=cb: q.tensor_tensor(out=cuh[:, cb, :], in0=hb[:, 0, :], in1=uh[:, :], op=ALU.mult), reads=[RB[7], Ruh], writes=[Rcuh])
            for g in range(NG):
                gsl = slice(g * 512, (g + 1) * 512)
                pb = []
                for part in range(4):
                    b = next_bank(7)
                    s = wsl[part]
                    for fc in range(FC):
                        p.add(PE, lambda q, b=b, s=s, fc=fc, gsl=gsl: q.matmul(banks[b][:, :], lhsT=W128[s][:, fc, :], rhs=xn2T[:, fc, gsl], start=(fc == 0), stop=(fc == FC - 1)),
                              reads=[RWr[s]] + [r for i in range(4) for r in Rxn[g * 4 + i]], writes=[RB[b]])
                    pb.append(b)
                r = gcnt % 2
                rp = (gcnt - 1) % 2
                gcnt += 1
                bB, bC, bu, bg = pb
                p.add(ACT, lambda q, r=r, bu=bu: q.copy(out=u_sb[r], in_=banks[bu][:, :]), reads=[RB[bu]], writes=[Ru[r]])
                p.add(DVE, lambda q, r=r, bC=bC: q.tensor_tensor(out=cu_ext[r][:, 2:514], in0=banks[bC][:, :], in1=u_sb[r], op=ALU.mult),
                      reads=[RB[bC], Ru[r]], writes=[Rcu[r]])
                if g == 0:
                    p.add(DVE, lambda q, r=r, cb=cb: q.tensor_copy(out=cu_ext[r][:, 0:2], in_=cuh[:, cb, :]), reads=[Rcuh], writes=[Rcu[r]])
                else:
                    p.add(DVE, lambda q, r=r, rp=rp: q.tensor_copy(out=cu_ext[r][:, 0:2], in_=cu_ext[rp][:, 512:514]), reads=[Rcu[rp]], writes=[Rcu[r]])
                p.add(DVE, lambda q, r=r, cb=cb: q.tensor_scalar(out=tcv[r], in0=cu_ext[r][:, 0:512], scalar1=convw[:, cb * 3:cb * 3 + 1], scalar2=None, op0=ALU.mult),
                      reads=[Rcu[r], Rcw], writes=[Rt[r]])
                for i in (1, 2):
                    p.add(DVE, lambda q, r=r, cb=cb, i=i: q.scalar_tensor_tensor(out=tcv[r], in0=cu_ext[r][:, i:i + 512], scalar=convw[:, cb * 3 + i:cb * 3 + i + 1],
                                                                              in1=tcv[r], op0=ALU.mult, op1=ALU.add),
                          reads=[Rcu[r], Rcw, Rt[r]], writes=[Rt[r]])
                p.add(ACT, lambda q, r=r, bg=bg: q.activation(out=sgb[r], in_=banks[bg][:, :], func=AF.Silu), reads=[RB[bg]], writes=[Rsg[r]])
                p.add(DVE, lambda q, r=r, bB=bB: q.tensor_tensor(out=t2b[r], in0=banks[bB][:, :], in1=tcv[r], op=ALU.mult), reads=[RB[bB], Rt[r]], writes=[Rt2[r]])
                p.add(POOL, lambda q, r=r, cb=cb, gsl=gsl: q.tensor_tensor(out=yT[:, cb, gsl], in0=t2b[r], in1=sgb[r], op=ALU.mult),
                      reads=[Rt2[r], Rsg[r]], writes=[RyT[g]])
        prog_barrier(p)

        RyTall = Res()
        junk2 = ar.view(r3, [512], F32)
        proj_pass(wco, lambda fc, tt: yT[:, fc, tt * 128:(tt + 1) * 128], RyTall,
                  lambda g, csl: h1s[2 + g * 512: 2 + (g + 1) * 512, csl].rearrange("(t p) c -> p t c", p=128),
                  lambda tt, csl: out[tt * 128:(tt + 1) * 128, csl], False, True)
        prog_barrier(p)

        Rgf = Res(); Rh2 = [Res(), Res()]
        p.dma(SP, lambda q: q.dma_start(out=gfinb, in_=gfin), writes=[Rgf])
        outs = []
        for t in range(NTT):
            hb_ = t % 2; sj = t % 4
            ssc = small[:, 2 * sj:2 * sj + 1]; rsc = small[:, 2 * sj + 1:2 * sj + 2]
            p.dma(SP, lambda q, t=t, hb_=hb_: q.dma_start(out=h2b[hb_], in_=out[t * 128:(t + 1) * 128, :]), writes=[Rh2[hb_]])
            p.add(DVE, lambda q, t=t, ssc=ssc: q.reduce_sum(out=ssc, in_=ss2[:, t * CG:(t + 1) * CG], axis=AX), writes=[Rst[sj]])
            p.add(ACT, lambda q, ssc=ssc, rsc=rsc: q.activation(out=rsc, in_=ssc, func=AF.Ln, scale=1.0 / D, bias=RMS_EPS), reads=[Rst[sj]], writes=[Rst[sj]])
            p.add(ACT, lambda q, rsc=rsc: q.activation(out=rsc, in_=rsc, func=AF.Exp, scale=-0.5), reads=[Rst[sj]], writes=[Rst[sj]])
            p.add(DVE, lambda q, hb_=hb_, rsc=rsc: q.scalar_tensor_tensor(out=h2b[hb_], in0=h2b[hb_], scalar=rsc, in1=gfinb, op0=ALU.mult, op1=ALU.mult),
                  reads=[Rh2[hb_], Rst[sj], Rgf], writes=[Rh2[hb_]])
            outs.append(p.dma(POOL, lambda q, t=t, hb_=hb_: q.dma_start(out=out[t * 128:(t + 1) * 128, :], in_=h2b[hb_]), reads=[Rh2[hb_]]))
        p.emit(final_wait_ops=outs[-6:])
    return nc


D_MODEL = 4096
SEQ = 8192
N_CORES = 8
FC_ = D_MODEL // 128
NH_CORE = 4
NT_CORE = SEQ // N_CORES

_NC_CACHE = {}


def _get_nc(name, fn):
    if name not in _NC_CACHE:
        _NC_CACHE[name] = fn()
    return _NC_CACHE[name]


def _lay512(W, FC):
    D = FC * 128
    CG = D // 512
    return np.ascontiguousarray(W.reshape(FC, 128, CG, 512).transpose(2, 1, 0, 3)).reshape(CG, 128, FC * 512)


def kernel(x, norm_attn, w_in_attn, w_out_attn, norm_conv, w_in_conv, conv_w, w_out_conv, final_norm):
    import ml_dtypes
    f32 = np.float32
    D, S, FC = D_MODEL, SEQ, FC_
    x2 = np.ascontiguousarray(np.asarray(x, dtype=f32).reshape(S, D))
    bc = lambda g: np.ascontiguousarray(np.broadcast_to(np.asarray(g, dtype=f32)[None, :], (128, D)))
    W4 = np.asarray(w_in_attn, dtype=f32).reshape(FC, 128, 4, 32, 128)
    gain_a = bc(norm_attn)
    wo = _lay512(np.asarray(w_out_attn, dtype=f32), FC)
    wco = _lay512(np.asarray(w_out_conv, dtype=f32), FC)
    wc = np.ascontiguousarray(np.asarray(w_in_conv, dtype=f32).reshape(FC, 128, 4, FC, 128).transpose(3, 2, 1, 0, 4)).reshape(FC, 4, 128, FC * 128)
    wflat = np.concatenate([wo.reshape(-1), wc.reshape(-1), wco.reshape(-1)])
    n_w = (wo.size, wc.size, wco.size)
    del wo, wc, wco
    per = wflat.size // N_CORES
    NCV = per // (128 * 2048)
    assert NCV * 128 * 2048 * N_CORES == wflat.size
    in_maps = []
    for c in range(N_CORES):
        wd = np.ascontiguousarray(W4[:, :, :, c * NH_CORE:(c + 1) * NH_CORE, :].transpose(3, 1, 0, 2, 4)).reshape(NH_CORE, 128, FC * 512)
        in_maps.append({"x": x2, "gain_b": gain_a, "w": wd, "wcv_in": wflat[c * per:(c + 1) * per].reshape(NCV * 128, 2048)})
    ncA = _get_nc("A", lambda: build_A(S, FC, NH_CORE, NCV))
    resA = run_bass_kernel_spmd(ncA, in_maps, core_ids=list(range(N_CORES)))
    wbf = np.concatenate([np.asarray(resA.results[c]["wcv_out"]).reshape(-1) for c in range(N_CORES)])
    wo = wbf[0:n_w[0]].reshape(FC * 128 // 512, 128, FC * 512)
    wc = wbf[n_w[0]:n_w[0] + n_w[1]].reshape(FC, 4, 128, FC * 128)
    wco = wbf[n_w[0] + n_w[1]:].reshape(FC * 128 // 512, 128, FC * 512)
    del wflat
    ogT = np.concatenate([np.asarray(resA.results[c]["og"]).reshape(NH_CORE * 128, S) for c in range(N_CORES)], axis=0)
    NT = NT_CORE
    cw = np.ascontiguousarray(np.asarray(conv_w, dtype=f32).reshape(3, FC, 128).transpose(2, 1, 0)).reshape(128, FC * 3)
    gconv = bc(norm_conv)
    gfin = bc(final_norm)
    in_maps = []
    for c in range(N_CORES):
        t0 = c * NT
        xh = np.zeros((NT + 2, D), dtype=f32)
        og_c = np.zeros((D, NT + 2), dtype=ogT.dtype)
        if c == 0:
            xh[2:] = x2[0:NT]
            og_c[:, 2:] = ogT[:, 0:NT]
        else:
            xh[:] = x2[t0 - 2:t0 + NT]
            og_c[:] = ogT[:, t0 - 2:t0 + NT]
        og_l = np.ascontiguousarray(og_c.reshape(FC, 128, NT + 2).transpose(1, 0, 2)).reshape(128, FC * (NT + 2))
        in_maps.append({"xh": xh, "ogT": og_l, "wo": wo, "gconv_b": gconv, "wc": wc, "convw": cw, "wco": wco, "gfin_b": gfin})
    ncB = _get_nc("B", lambda: build_B(NT, FC))
    resB = run_bass_kernel_spmd(ncB, in_maps, core_ids=list(range(N_CORES)))
    out = np.concatenate([np.asarray(resB.results[c]["out"], dtype=f32) for c in range(N_CORES)], axis=0)
    return out.reshape(1, S, D)
```

```python
import numpy as np
import concourse.bass as bass
import concourse.mybir as mybir
from concourse.bass_utils import run_bass_kernel_spmd

F32 = mybir.dt.float32
BF16 = mybir.dt.bfloat16
AF = mybir.ActivationFunctionType
ALU = mybir.AluOpType

PE, ACT, DVE, POOL, SP = "tensor", "scalar", "vector", "gpsimd", "sync"
ENGS = (PE, ACT, DVE, POOL, SP)


class Res:
    __slots__ = ("writer", "readers", "name")

    def __init__(self, name=""):
        self.writer = None
        self.readers = {}
        self.name = name


class Op:
    __slots__ = ("eng", "fn", "deps", "is_dma", "sem", "val", "needed", "idx")

    def __init__(self, eng, fn, deps, is_dma):
        self.eng = eng
        self.fn = fn
        self.deps = deps
        self.is_dma = is_dma
        self.sem = None
        self.val = None
        self.needed = False
        self.idx = None


class Prog:
    def __init__(self, nc, same_engine_sync=True, dma_ring=6):
        self.nc = nc
        self.ops = {e: [] for e in ENGS}
        self.same_engine_sync = same_engine_sync
        self.dma_ring = dma_ring
        self.dma_hist = {e: [] for e in ENGS}
        self.n_ops = 0
        self.barrier_deps = []
        self.barrier_seen = set()

    def _deps(self, reads, writes, extra):
        deps = []
        for r in reads:
            if r.writer is not None:
                deps.append(r.writer)
        for w in writes:
            if w.writer is not None:
                deps.append(w.writer)
            deps.extend(w.readers.values())
        deps.extend(extra)
        return deps

    def _bar(self, eng, deps):
        if self.barrier_deps and eng not in self.barrier_seen:
            self.barrier_seen.add(eng)
            deps.extend(self.barrier_deps)
        return deps

    def _commit(self, op, reads, writes):
        for r in reads:
            if op.is_dma:
                r.readers[("dma", id(op))] = op
            else:
                r.readers[op.eng] = op
        for w in writes:
            w.writer = op
            w.readers = {}

    def add(self, eng, fn, reads=(), writes=(), extra=()):
        op = Op(eng, fn, self._bar(eng, self._deps(reads, writes, extra)), False)
        self._commit(op, reads, writes)
        self.ops[eng].append(op)
        self.n_ops += 1
        return op

    def dma(self, eng, fn, reads=(), writes=(), extra=()):
        deps = self._bar(eng, self._deps(reads, writes, extra))
        hist = self.dma_hist[eng]
        if len(hist) >= self.dma_ring:
            deps.append(hist[-self.dma_ring])
        op = Op(eng, fn, deps, True)
        op.idx = len(hist)
        hist.append(op)
        self._commit(op, reads, writes)
        self.ops[eng].append(op)
        self.n_ops += 1
        return op

    def emit(self, final_wait_ops=()):
        nc = self.nc
        for e in ENGS:
            for op in self.ops[e]:
                for d in op.deps:
                    if d.is_dma or d.eng != op.eng or (self.same_engine_sync and op.eng != PE) or op.is_dma:
                        d.needed = True
        for op in final_wait_ops:
            op.needed = True
        import contextlib
        with contextlib.ExitStack() as st:
            eng_sem = {e: st.enter_context(nc.semaphore("p_" + e)) for e in (PE, ACT, DVE, POOL)}
            dma_sems = {e: [st.enter_context(nc.semaphore("d_%s_%d" % (e, i))) for i in range(self.dma_ring)]
                        for e in ENGS if self.dma_hist[e]}
            for e in ENGS:
                cnt = 0
                ring_cnt = [0] * self.dma_ring
                for op in self.ops[e]:
                    if op.is_dma:
                        s = op.idx % self.dma_ring
                        ring_cnt[s] += 16
                        op.sem = dma_sems[e][s]
                        op.val = ring_cnt[s]
                    elif op.needed:
                        cnt += 1
                        op.sem = eng_sem[e]
                        op.val = cnt
            block = st.enter_context(nc.Block())
            for e in ENGS:
                ops = self.ops[e]
                if not ops and not (e == SP and final_wait_ops):
                    continue
                tail = list(final_wait_ops) if e == SP else []

                def body(eng, ops=ops, e=e, tail=tail):
                    waited = {}
                    def do_wait(d):
                        if d.sem is None:
                            return
                        k = id(d.sem)
                        if waited.get(k, 0) >= d.val:
                            return
                        eng.wait_ge(d.sem, d.val)
                        waited[k] = d.val
                    for op in ops:
                        best = {}
                        for d in op.deps:
                            if (not d.is_dma) and d.eng == e and not op.is_dma:
                                if e == PE or not self.same_engine_sync:
                                    continue
                            if d.sem is None:
                                continue
                            k = id(d.sem)
                            if k not in best or best[k].val < d.val:
                                best[k] = d
                        for d in best.values():
                            do_wait(d)
                        ins = op.fn(eng)
                        if op.is_dma:
                            ins.then_inc(op.sem, 16)
                        elif op.needed:
                            ins.then_inc(op.sem, 1)
                    for d in tail:
                        do_wait(d)
                getattr(block, e)(body)


def prog_barrier(p):
    deps = []
    for e in ENGS:
        ops = p.ops[e]
        comp = [o for o in ops if not o.is_dma]
        if comp:
            deps.append(comp[-1])
        deps.extend(p.dma_hist[e][-p.dma_ring:])
    p.barrier_deps = deps
    p.barrier_seen = set()


class Arena:
    def __init__(self, t, nbytes):
        self.t = t
        self.n = nbytes
        self.top = 0

    def alloc(self, nbytes, align=64):
        off = (self.top + align - 1) // align * align
        assert off + nbytes <= self.n, ("arena overflow", off, nbytes, self.n)
        self.top = off + nbytes
        return off

    def view(self, off, shape, dt, parts=128):
        esz = 2 if dt == BF16 else 4
        n = int(np.prod(shape))
        ap = self.t[0:parts, off // 2: off // 2 + n * esz // 2]
        if dt != BF16:
            ap = ap.bitcast(dt)
        if len(shape) == 2:
            ap = ap.rearrange("p (a b) -> p a b", a=shape[0])
        elif len(shape) == 3:
            ap = ap.rearrange("p (a b c) -> p a b c", a=shape[0], b=shape[1])
        return ap


import contextlib

RMS_EPS = 1e-6
KIB = 1024


def _consts(nc, st, p):
    ident = st.enter_context(nc.sbuf_tensor("ident", [128, 128], BF16))
    U = st.enter_context(nc.sbuf_tensor("Utri", [128, 128], BF16))
    NU = st.enter_context(nc.sbuf_tensor("NUtri", [128, 128], BF16))
    Rc = Res("consts")
    p.add(POOL, lambda q: q.memset(ident[:], 1.0), writes=[Rc])
    p.add(POOL, lambda q: q.affine_select(out=ident[:], in_=ident[:], pattern=[[-1, 128]], compare_op=ALU.is_equal,
                                          fill=0.0, base=0, channel_multiplier=1), writes=[Rc])
    p.add(POOL, lambda q: q.memset(U[:], 1.0), writes=[Rc])
    p.add(POOL, lambda q: q.affine_select(out=U[:], in_=U[:], pattern=[[-1, 128]], compare_op=ALU.is_ge,
                                          fill=0.0, base=0, channel_multiplier=1), writes=[Rc])
    p.add(POOL, lambda q: q.memset(NU[:], 1.0), writes=[Rc])
    p.add(POOL, lambda q: q.affine_select(out=NU[:], in_=NU[:], pattern=[[1, 128]], compare_op=ALU.is_gt,
                                          fill=0.0, base=0, channel_multiplier=-1), writes=[Rc])
    return ident, U, NU, Rc


def _prep_tile(p, nc, src_ap, npart, D, x32, Rx32, junk, Rjunk, ss, rs, Rst, gainb, Rgain, xs, Rxs,
               ident, Rc, banks_bf, RB, bank_ids, dst_fn, Rdst, ev_flip, phase=0):
    if phase in (0, 1):
        _prep_tile_a(p, nc, src_ap, npart, D, x32, Rx32, junk, Rjunk, ss, rs, Rst, gainb, Rgain, xs, Rxs)
    if phase in (0, 2):
        _prep_tile_b(p, npart, D, xs, Rxs, ident, Rc, banks_bf, RB, bank_ids, dst_fn, Rdst, ev_flip)


def _prep_tile_a(p, nc, src_ap, npart, D, x32, Rx32, junk, Rjunk, ss, rs, Rst, gainb, Rgain, xs, Rxs):
    FC = D // 128
    p.dma(SP, lambda q: q.dma_start(out=x32[0:npart, :], in_=src_ap), writes=[Rx32])
    p.add(ACT, lambda q: q.activation(out=junk[0:npart, :], in_=x32[0:npart, :], func=AF.Square, accum_out=ss[0:npart, :]),
          reads=[Rx32], writes=[Rjunk, Rst])
    p.add(ACT, lambda q: q.activation(out=rs[0:npart, :], in_=ss[0:npart, :], func=AF.Ln, scale=1.0 / D, bias=RMS_EPS),
          reads=[Rst], writes=[Rst])
    p.add(ACT, lambda q: q.activation(out=rs[0:npart, :], in_=rs[0:npart, :], func=AF.Exp, scale=-0.5), reads=[Rst], writes=[Rst])
    p.add(DVE, lambda q: q.scalar_tensor_tensor(out=xs[0:npart, :], in0=x32[0:npart, :], scalar=rs[0:npart, 0:1],
                                                in1=gainb[0:npart, :], op0=ALU.mult, op1=ALU.mult),
          reads=[Rx32, Rst, Rgain], writes=[Rxs])


def _prep_tile_b(p, npart, D, xs, Rxs, ident, Rc, banks_bf, RB, bank_ids, dst_fn, Rdst, ev_flip):
    FC = D // 128
    nb = (FC + 7) // 8
    for bi in range(nb):
        b = bank_ids[bi]
        c0 = bi * 8
        n = min(8, FC - c0)
        for c in range(c0, c0 + n):
            p.add(PE, lambda q, c=c, b=b, c0=c0: q.transpose(out=banks_bf[b][:, c - c0, 0:npart], in_=xs[0:npart, c * 128:(c + 1) * 128],
                                                         identity=ident[0:npart, 0:npart]),
                  reads=[Rxs, Rc], writes=[RB[b]])
        eng = ACT if (bi + ev_flip) % 2 == 0 else DVE
        rd = Rdst[bi] if isinstance(Rdst, (list, tuple)) else Rdst
        if eng == ACT:
            p.add(ACT, lambda q, b=b, c0=c0, n=n: q.copy(out=dst_fn(c0, n), in_=banks_bf[b][:, 0:n, 0:npart]),
                  reads=[RB[b]], writes=[rd])
        else:
            p.add(DVE, lambda q, b=b, c0=c0, n=n: q.tensor_copy(out=dst_fn(c0, n), in_=banks_bf[b][:, 0:n, 0:npart]),
                  reads=[RB[b]], writes=[rd])


def build_A(S, FC, NH, NCV=0):
    D = FC * 128
    NTG = S // 512
    NKB = S // 128
    scale = 1.0 / (128 ** 0.5)
    nc = bass.Bass("TRN2", target_bir_lowering=False)
    x = nc.dram_tensor("x", [S, D], F32, kind="ExternalInput").ap()
    gain_d = nc.dram_tensor("gain_b", [128, D], F32, kind="ExternalInput").ap()
    w = nc.dram_tensor("w", [NH, 128, FC * 512], F32, kind="ExternalInput").ap()
    og = nc.dram_tensor("og", [NH, 128, S], BF16, kind="ExternalOutput").ap()
    scr = nc.dram_tensor("xT_scr", [NTG, 128, FC * 512], BF16, kind="Internal").ap()
    if NCV:
        wcv_in = nc.dram_tensor("wcv_in", [NCV * 128, 2048], F32, kind="ExternalInput").ap()
        wcv_out = nc.dram_tensor("wcv_out", [NCV * 128, 2048], BF16, kind="ExternalOutput").ap()
    cv_next = [0]

    with contextlib.ExitStack() as st:
        p = Prog(nc)
        ident, U, NU, Rc = _consts(nc, st, p)
        small = st.enter_context(nc.sbuf_tensor("small", [128, 16], F32))
        NX32 = 4
        szP = D * 4 + NX32 * D * 4 + 2 * D * 2 + 2 * FC * 1024
        szH = 4 * S * 2 + FC * 1024 + 2 * FC * 1024
        ring_sz = 5 * 2048 + 4 * 1024 + 3 * 2048 + 3 * 1024 + 2 * 1024 + 2 * 2048 + 2 * 1024
        tot = max(szP, szH) + ring_sz + 1024
        art = st.enter_context(nc.sbuf_tensor("arena", [128, tot // 2], BF16))
        ar = Arena(art, tot)
        base = ar.alloc(max(szP, szH))
        o = base
        gainb = ar.view(o, [D], F32); o += D * 4
        x32 = []
        for i in range(NX32):
            x32.append(ar.view(o, [D], F32)); o += D * 4
        xs = []
        for i in range(2):
            xs.append(ar.view(o, [D], BF16)); o += D * 2
        xTg, xTg_flat = [], []
        for i in range(2):
            xTg.append(ar.view(o, [FC, 512], BF16)); xTg_flat.append(ar.view(o, [FC * 512], BF16)); o += FC * 1024
        o = base
        qT = ar.view(o, [S], BF16); o += S * 2
        sgT = ar.view(o, [S], BF16); o += S * 2
        kT = ar.view(o, [S], BF16); o += S * 2
        vsb = ar.view(o, [NKB, 128], BF16); o += S * 2
        Wh = ar.view(o, [FC, 512], BF16); Wh_flat = ar.view(o, [FC * 512], BF16); o += FC * 1024
        xin, xin_flat = [], []
        for i in range(2):
            xin.append(ar.view(o, [FC, 512], BF16)); xin_flat.append(ar.view(o, [FC * 512], BF16)); o += FC * 1024
        def ring(n, dt):
            esz = 4 if dt == F32 else 2
            return [ar.view(ar.alloc(512 * esz), [512], dt) for _ in range(n)]
        e_r = ring(5, F32); L_r = ring(4, BF16); w_r = ring(3, F32); a_r = ring(3, BF16)
        vst = ring(2, BF16); gtm = ring(2, F32); ogs = ring(2, BF16)
        Re = [Res() for _ in e_r]; RL = [Res() for _ in L_r]; Rw = [Res() for _ in w_r]; Ra = [Res() for _ in a_r]
        Rvst = [Res() for _ in vst]; Rgtm = [Res() for _ in gtm]; Rogs = [Res() for _ in ogs]
        masks = st.enter_context(nc.sbuf_tensor("masks", [128, 4, 512], F32))
        Rmask = Res()
        p.add(POOL, lambda q: q.memset(masks[:, :, :], 1.0), writes=[Rmask])
        for i_ in range(4):
            p.add(POOL, lambda q, i_=i_: q.affine_select(out=masks[:, i_, :], in_=masks[:, i_, :], pattern=[[1, 512]], compare_op=ALU.is_gt,
                                                         fill=0.0, base=-(3 - i_) * 128, channel_multiplier=-1), writes=[Rmask])
        banks = [st.enter_context(nc.psum_tensor("bank%d" % i, [128, 512], F32)) for i in range(8)]
        banks_bf = [b.bitcast(BF16)[:, :].rearrange("p (a b) -> p a b", a=8) for b in banks]
        RB = [Res("bank%d" % i) for i in range(8)]

        Rgain = Res(); Rx32 = [Res() for _ in range(NX32)]; Rxs = [Res(), Res()]
        Rst = [Res() for _ in range(4)]
        RxTg = [[[Res() for _ in range(4)] for _ in range(4)] for _ in range(2)]
        Rscr = [Res() for _ in range(NTG)]
        p.dma(SP, lambda q: q.dma_start(out=gainb, in_=gain_d), writes=[Rgain])
        nbk = (FC + 7) // 8
        def stageP(t, phase):
            tg, tt = t // 4, t % 4
            bsel = tg % 2
            xb = t % 2
            x4 = t % NX32
            sj = t % 4
            bank_ids = [((t % 2) * 4 + i) % 8 for i in range(nbk)]
            _prep_tile(p, nc, x[t * 128:(t + 1) * 128, :], 128, D, x32[x4], Rx32[x4], xs[xb], Rxs[xb],
                       small[:, 2 * sj:2 * sj + 1], small[:, 2 * sj + 1:2 * sj + 2], Rst[sj], gainb, Rgain,
                       xs[xb], Rxs[xb], ident, Rc, banks_bf, RB, bank_ids,
                       lambda c0, n, bsel=bsel, tt=tt: xTg[bsel][:, c0:c0 + n, tt * 128:(tt + 1) * 128],
                       RxTg[bsel][tt], t, phase=phase)
            if phase == 2 and tt == 3:
                p.dma(POOL, lambda q, tg=tg, bsel=bsel: q.dma_start(out=scr[tg], in_=xTg_flat[bsel]),
                      reads=[r for rr in RxTg[bsel] for r in rr], writes=[Rscr[tg]])
        NT_ = NTG * 4
        stageP(0, 1)
        for t in range(NT_):
            if t + 1 < NT_:
                stageP(t + 1, 1)
            stageP(t, 2)
        prog_barrier(p)

        RW = Res(); Rxin = [Res(), Res()]
        RqT = [Res() for _ in range(NTG)]; RsgT = [Res() for _ in range(NTG)]
        RkT = [Res() for _ in range(NTG)]; Rv = [Res() for _ in range(NTG)]
        out_ops = []
        og_ops = []
        bank_rr = [0]
        WCH = 2048
        IPB = [5, 6]
        TB = 7
        ZB = [0, 1]; CB = 2; OB = [3, 4]
        CHUNK = 4
        NCH = 4 * (FC // CHUNK) + 1

        def load_W(h):
            for k in range(FC * 512 // WCH):
                p.dma(POOL, lambda q, h=h, k=k: q.dma_start(out=Wh_flat[:, k * WCH:(k + 1) * WCH], in_=w[h][:, k * WCH:(k + 1) * WCH]),
                      writes=[RW])

        def load_xin(tg):
            xb = tg % 2
            p.dma(SP, lambda q, tg=tg, xb=xb: q.dma_start(out=xin_flat[xb], in_=scr[tg]), reads=[Rscr[tg]], writes=[Rxin[xb]])

        def unit_gen(h, tg):
            xb = tg % 2
            tsl = slice(tg * 512, (tg + 1) * 512)
            for j in (1, 2, 0, 3):
                b = IPB[bank_rr[0] % 2]
                bank_rr[0] += 1
                for c in range(FC):
                    p.add(PE, lambda q, b=b, c=c, j=j, xb=xb: q.matmul(banks[b][:, :], lhsT=Wh[:, c, j * 128:(j + 1) * 128], rhs=xin[xb][:, c, :],
                                                                     start=(c == 0), stop=(c == FC - 1)),
                          reads=[RW, Rxin[xb]], writes=[RB[b]])
                    if c % CHUNK == CHUNK - 1 and c != FC - 1:
                        yield 1
                if j == 0:
                    p.add(DVE, lambda q, b=b, tsl=tsl: q.tensor_copy(out=qT[:, tsl], in_=banks[b][:, :]), reads=[RB[b]], writes=[RqT[tg]])
                elif j == 1:
                    p.add(DVE, lambda q, b=b, tsl=tsl: q.tensor_copy(out=kT[:, tsl], in_=banks[b][:, :]), reads=[RB[b]], writes=[RkT[tg]])
                elif j == 2:
                    r = tg % 2
                    p.add(DVE, lambda q, b=b, r=r: q.tensor_copy(out=vst[r], in_=banks[b][:, :]), reads=[RB[b]], writes=[Rvst[r]])
                    for i in range(4):
                        p.add(PE, lambda q, r=r, i=i: q.transpose(out=banks_bf[TB][:, i, :], in_=vst[r][:, i * 128:(i + 1) * 128], identity=ident[:, :]),
                              reads=[Rvst[r], Rc], writes=[RB[TB]])
                    p.add(ACT, lambda q, tg=tg: q.copy(out=vsb[:, tg * 4:(tg + 1) * 4, :], in_=banks_bf[TB][:, 0:4, :]),
                          reads=[RB[TB]], writes=[Rv[tg]])
                else:
                    r = tg % 2
                    p.add(ACT, lambda q, b=b, r=r: q.activation(out=gtm[r], in_=banks[b][:, :], func=AF.Exp, scale=-1.0),
                          reads=[RB[b]], writes=[Rgtm[r]])
                    p.add(DVE, lambda q, r=r: q.tensor_scalar(out=gtm[r], in0=gtm[r], scalar1=1.0, scalar2=None, op0=ALU.add),
                          reads=[Rgtm[r]], writes=[Rgtm[r]])
                    p.add(DVE, lambda q, r=r: q.reciprocal(out=gtm[r], in_=gtm[r]), reads=[Rgtm[r]], writes=[Rgtm[r]])
                    p.add(DVE, lambda q, b=b, r=r, tsl=tsl: q.tensor_tensor(out=sgT[:, tsl], in0=banks[b][:, :], in1=gtm[r], op=ALU.mult),
                          reads=[RB[b], Rgtm[r]], writes=[RsgT[tg]])
                yield 1
            for _ in range(2):
                if cv_next[0] < NCV:
                    k = cv_next[0]
                    cv_next[0] += 1
                    out_ops.append(p.dma(POOL, lambda q, k=k: q.dma_start(out=wcv_out[k * 128:(k + 1) * 128, :], in_=wcv_in[k * 128:(k + 1) * 128, :])))

        def attention(h):
            st_ = {"done": 0, "gen": None, "left": 0}

            def start_unit():
                tg = st_["done"]
                st_["gen"] = unit_gen(h, tg)
                st_["left"] = NCH

            def advance(n):
                for _ in range(n):
                    if st_["gen"] is None:
                        if st_["done"] >= NTG:
                            return
                        start_unit()
                    if next(st_["gen"], None) is None:
                        st_["gen"] = None
                        st_["done"] += 1
                        nxt = st_["done"] + 1
                        if nxt < NTG:
                            load_xin(nxt)
                        return
                    st_["left"] -= 1

            def pump_until(n_units):
                while st_["done"] < min(n_units, NTG):
                    advance(1)

            load_xin(0)
            if NTG > 1:
                load_xin(1)
            pump_until(1)

            steps = []
            first_step = {}
            for T in range(NTG):
                n = 4 * T + 4
                first_step[T] = len(steps)
                for i in range(n):
                    steps.append((T, i, 4 * T + 3 - i, i == n - 1))
            N = len(steps)

            def valid(j):
                return 0 <= j < N
            for j in range(-3, N + 2):
                if valid(j - 1):
                    wr = (j - 1) % 3
                    p.add(ACT, lambda q, wr=wr: q.activation(out=w_r[wr], in_=banks[CB][:, :], func=AF.Exp, scale=-1.0),
                          reads=[RB[CB]], writes=[Rw[wr]])
                if valid(j + 2):
                    T, i, kb, last = steps[j + 2]
                    zb = ZB[(j + 2) % 2]; er = (j + 2) % 5
                    p.add(ACT, lambda q, zb=zb, er=er: q.activation(out=e_r[er], in_=banks[zb][:, :], func=AF.Exp, scale=scale),
                          reads=[RB[zb]], writes=[Re[er]])
                    if i < 4:
                        p.add(DVE, lambda q, er=er, i=i: q.tensor_tensor(out=e_r[er], in0=e_r[er], in1=masks[:, i, :], op=ALU.mult),
                              reads=[Re[er], Rmask], writes=[Re[er]])
                if valid(j + 1):
                    er = (j + 1) % 5; lr = (j + 1) % 4
                    p.add(ACT, lambda q, er=er, lr=lr: q.activation(out=L_r[lr], in_=e_r[er], func=AF.Ln, bias=1.0),
                          reads=[Re[er]], writes=[RL[lr]])
                if valid(j):
                    T, i, kb, last = steps[j]
                    if st_["done"] < NTG and st_["done"] <= T + 1:
                        iters_left = max(1, (4 * T + 4) - i - 3)
                        if st_["gen"] is None:
                            start_unit()
                        advance(-(-st_["left"] // iters_left))
                if valid(j):
                    T, i, kb, last = steps[j]
                    lr = j % 4; lp = (j - 1) % 4
                    if i > 0:
                        p.add(PE, lambda q, lp=lp: q.matmul(banks[CB][:, :], lhsT=NU[:, :], rhs=L_r[lp], start=False, stop=False, skip_group_check=True),
                              reads=[Rc, RL[lp]], writes=[RB[CB]])
                    p.add(PE, lambda q, lr=lr, i=i: q.matmul(banks[CB][:, :], lhsT=U[:, :], rhs=L_r[lr], start=(i == 0), stop=True, skip_group_check=(i > 0)),
                          reads=[Rc, RL[lr]], writes=[RB[CB]])
                if valid(j + 3):
                    T, i, kb, last = steps[j + 3]
                    if i == 0:
                        pump_until(T + 1)
                    zb = ZB[(j + 3) % 2]
                    p.add(PE, lambda q, zb=zb, kb=kb, T=T: q.matmul(banks[zb][:, :], lhsT=kT[:, kb * 128:(kb + 1) * 128], rhs=qT[:, T * 512:(T + 1) * 512],
                                                                start=True, stop=True),
                          reads=[RkT[kb // 4], RqT[T]], writes=[RB[zb]])
                if valid(j - 1):
                    T, i, kb, last = steps[j - 1]
                    er = (j - 1) % 5; wr = (j - 1) % 3; ai = (j - 1) % 3
                    ob = OB[T % 2]
                    p.add(DVE, lambda q, er=er, wr=wr, ai=ai: q.tensor_tensor(out=a_r[ai], in0=e_r[er], in1=w_r[wr], op=ALU.mult),
                          reads=[Re[er], Rw[wr]], writes=[Ra[ai]])
                    p.add(PE, lambda q, ob=ob, kb=kb, ai=ai, i=i, last=last: q.matmul(banks[ob][:, :], lhsT=vsb[:, kb, :], rhs=a_r[ai], start=(i == 0), stop=last),
                          reads=[Rv[kb // 4], Ra[ai]], writes=[RB[ob]])
                    if last:
                        sr = T % 2
                        p.add(DVE, lambda q, ob=ob, sr=sr, T=T: q.tensor_tensor(out=ogs[sr], in0=banks[ob][:, :], in1=sgT[:, T * 512:(T + 1) * 512], op=ALU.mult),
                              reads=[RB[ob], RsgT[T]], writes=[Rogs[sr]])
                        og_ops.append(p.dma(SP, lambda q, h=h, sr=sr, T=T: q.dma_start(out=og[h][:, T * 512:(T + 1) * 512], in_=ogs[sr]),
                                            reads=[Rogs[sr]]))
                    if T == NTG - 1 and i == 0 and h + 1 < NH:
                        load_W(h + 1)
            pump_until(NTG)

        load_W(0)
        for h in range(NH):
            attention(h)

        p.emit(final_wait_ops=out_ops[-6:] + og_ops[-6:] + p.dma_hist[SP][-6:])
    return nc


def build_B(NT, FC):
    D = FC * 128
    CG = D // 512
    NTT = NT // 128
    NG = NT // 512
    NH4 = NTT // 4
    AX = mybir.AxisListType.X
    nc = bass.Bass("TRN2", target_bir_lowering=False)
    xh = nc.dram_tensor("xh", [NT + 2, D], F32, kind="ExternalInput").ap()
    ogT_d = nc.dram_tensor("ogT", [128, FC * (NT + 2)], BF16, kind="ExternalInput").ap()
    wo = nc.dram_tensor("wo", [CG, 128, FC * 512], BF16, kind="ExternalInput").ap()
    gconv = nc.dram_tensor("gconv_b", [128, D], F32, kind="ExternalInput").ap()
    wc = nc.dram_tensor("wc", [FC, 4, 128, FC * 128], BF16, kind="ExternalInput").ap()
    convw_d = nc.dram_tensor("convw", [128, FC * 3], F32, kind="ExternalInput").ap()
    wco = nc.dram_tensor("wco", [CG, 128, FC * 512], BF16, kind="ExternalInput").ap()
    gfin = nc.dram_tensor("gfin_b", [128, D], F32, kind="ExternalInput").ap()
    out = nc.dram_tensor("out", [NT, D], F32, kind="ExternalOutput").ap()
    h1s = nc.dram_tensor("h1s", [NT + 2, D], F32, kind="Internal").ap()

    with contextlib.ExitStack() as st:
        p = Prog(nc)
        ident, U, NU, Rc = _consts(nc, st, p)
        small = st.enter_context(nc.sbuf_tensor("small", [128, 16], F32))
        convw = st.enter_context(nc.sbuf_tensor("convw_sb", [128, FC * 3], F32))
        cuh = st.enter_context(nc.sbuf_tensor("cuh", [128, FC, 2], F32))
        uh = st.enter_context(nc.sbuf_tensor("uh", [128, 2], F32))
        ss2 = st.enter_context(nc.sbuf_tensor("ss2", [128, NTT * CG], F32))
        xn2Th = st.enter_context(nc.sbuf_tensor("xn2Th", [128, FC, 2], BF16))
        szR1 = max(FC * (NT + 2) * 2, 2 * D * 4 + D * 2 + D * 2 + D * 4, FC * NT * 2)
        szR1 = (szR1 + 63) // 64 * 64
        szR3 = max(FC * NT * 2, 3 * D * 4)
        WCH = 8192
        NW = 5
        szR2 = 3 * 8192
        tot = szR1 + szR3 + NW * WCH + szR2 + 4 * 2048 + 1024
        art = st.enter_context(nc.sbuf_tensor("arena", [128, tot // 2], BF16))
        ar = Arena(art, tot)
        r1 = ar.alloc(szR1); r3 = ar.alloc(szR3); wr = ar.alloc(NW * WCH); r2 = ar.alloc(szR2); sg_off = ar.alloc(4 * 2048)
        ogT = ar.view(r1, [FC, NT + 2], BF16); ogT_flat = ar.view(r1, [FC * (NT + 2)], BF16)
        o = r1
        x32 = []
        for i in range(2):
            x32.append(ar.view(o, [D], F32)); o += D * 4
        junk = ar.view(o, [D], BF16); o += D * 2
        xs1 = ar.view(o, [D], BF16); o += D * 2
        gainb = ar.view(o, [D], F32); o += D * 4
        yT = ar.view(r1, [FC, NT], BF16)
        xn2T = ar.view(r3, [FC, NT], BF16)
        h2b = [ar.view(r3 + i * D * 4, [D], F32) for i in range(2)]
        gfinb = ar.view(r3 + 2 * D * 4, [D], F32)
        Wfl = [ar.view(wr + i * WCH, [WCH // 2], BF16) for i in range(NW)]
        W512 = [ar.view(wr + i * WCH, [8, 512], BF16) for i in range(NW)]
        W128 = [ar.view(wr + i * WCH, [FC, 128], BF16) for i in range(NW)] if FC * 128 * 2 <= WCH else None
        RWr = [Res() for _ in range(NW)]
        wcnt = [0]
        colt = [ar.view(r2 + i * 8192, [4, 512], F32) for i in range(3)]
        Rcolt = [Res() for _ in range(3)]
        gt = [ar.view(r2 + i * 2048, [512], F32) for i in range(10)]
        cu_ext = [ar.view(r2 + 0, [1024], F32), ar.view(r2 + 4096, [1024], F32)]
        u_sb = [gt[4], gt[5]]; tcv = [gt[6], gt[7]]; sgb = [gt[8], gt[9]]; t2b = [gt[10 - 10 + 0 + 0] if False else ar.view(r2 + 20480, [512], F32), ar.view(r2 + 22528, [512], F32)]
        Rcu = [Res(), Res()]; Ru = [Res(), Res()]; Rt = [Res(), Res()]; Rsg = [Res(), Res()]; Rt2 = [Res(), Res()]
        stg = [ar.view(sg_off + i * 2048, [512], F32) for i in range(4)]
        Rstg = [Res() for _ in range(4)]
        scnt = [0]
        banks = [st.enter_context(nc.psum_tensor("bank%d" % i, [128, 512], F32)) for i in range(8)]
        banks_bf = [b.bitcast(BF16)[:, :].rearrange("p (a b) -> p a b", a=8) for b in banks]
        RB = [Res("bank%d" % i) for i in range(8)]
        bcnt = [0]

        def next_bank(n=7):
            b = bcnt[0] % n
            bcnt[0] += 1
            return b

        def load_w512(src_rows):
            s = wcnt[0] % NW
            wcnt[0] += 1
            p.dma(SP, lambda q, s=s: q.dma_start(out=Wfl[s], in_=src_rows), writes=[RWr[s]])
            return s

        RogT = Res()
        nsp = 4
        tot_og = FC * (NT + 2)
        step_og = (tot_og + nsp - 1) // nsp
        for k in range(nsp):
            a0, a1 = k * step_og, min(tot_og, (k + 1) * step_og)
            p.dma(SP, lambda q, a0=a0, a1=a1: q.dma_start(out=ogT_flat[:, a0:a1], in_=ogT_d[:, a0:a1]), writes=[RogT])
        Rh1 = Res()

        def proj_pass(w_dram, lhs_fn, lhs_res, add_src_fn, dst_fn, halo, sumsq):
            colcnt = 0
            for cg in range(CG):
                slots = [load_w512(w_dram[cg][:, k * 4096:(k + 1) * 4096]) for k in range(FC // 8)]
                csl = slice(cg * 512, (cg + 1) * 512)
                cts = []
                for g in range(NH4):
                    ci = colcnt % 3
                    colcnt += 1
                    src = add_src_fn(g, csl)
                    p.dma(SP, lambda q, ci=ci, src=src: q.dma_start(out=colt[ci], in_=src), writes=[Rcolt[ci]])
                    cts.append(ci)
                for tt in range(NTT):
                    b = next_bank(7)
                    for fc in range(FC):
                        s = slots[fc // 8]
                        p.add(PE, lambda q, b=b, fc=fc, s=s, tt=tt: q.matmul(banks[b][:, :], lhsT=lhs_fn(fc, tt), rhs=W512[s][:, fc % 8, :],
                                                                           start=(fc == 0), stop=(fc == FC - 1)),
                              reads=[lhs_res, RWr[s]], writes=[RB[b]])
                    si = scnt[0] % 4
                    scnt[0] += 1
                    ci = cts[tt // 4]
                    p.add(DVE, lambda q, b=b, si=si, ci=ci, tt=tt: q.tensor_tensor(out=stg[si], in0=banks[b][:, :], in1=colt[ci][:, tt % 4, :], op=ALU.add),
                          reads=[RB[b], Rcolt[ci]], writes=[Rstg[si]])
                    if sumsq:
                        col = tt * CG + cg
                        p.add(ACT, lambda q, si=si, col=col: q.activation(out=junk2, in_=stg[si], func=AF.Square, accum_out=ss2[:, col:col + 1]),
                              reads=[Rstg[si]], writes=[Rjunk2])
                    p.dma(POOL, lambda q, si=si, tt=tt, csl=csl: q.dma_start(out=dst_fn(tt, csl), in_=stg[si]), reads=[Rstg[si]], writes=[Rh1])
                if halo:
                    b = 7
                    for fc in range(FC):
                        s = slots[fc // 8]
                        p.add(PE, lambda q, fc=fc, s=s: q.matmul(banks[b][0:2, :], lhsT=ogT[:, fc, 0:2], rhs=W512[s][:, fc % 8, :],
                                                               start=(fc == 0), stop=(fc == FC - 1)),
                              reads=[lhs_res, RWr[s]], writes=[RB[b]])
                    si = scnt[0] % 4
                    scnt[0] += 1
                    ci = colcnt % 3
                    colcnt += 1
                    p.dma(SP, lambda q, ci=ci, csl=csl: q.dma_start(out=colt[ci][0:2, 0, :], in_=xh[0:2, csl]), writes=[Rcolt[ci]])
                    p.add(DVE, lambda q, si=si, ci=ci: q.tensor_tensor(out=stg[si][0:2, :], in0=banks[b][0:2, :], in1=colt[ci][0:2, 0, :], op=ALU.add),
                          reads=[RB[b], Rcolt[ci]], writes=[Rstg[si]])
                    p.dma(POOL, lambda q, si=si, csl=csl: q.dma_start(out=h1s[0:2, csl], in_=stg[si][0:2, :]), reads=[Rstg[si]], writes=[Rh1])

        junk2 = ar.view(sg_off, [512], F32)
        Rjunk2 = Res()
        proj_pass(wo, lambda fc, tt: ogT[:, fc, 2 + tt * 128: 2 + (tt + 1) * 128], RogT,
                  lambda g, csl: xh[2 + g * 512: 2 + (g + 1) * 512, csl].rearrange("(t p) c -> p t c", p=128),
                  lambda tt, csl: h1s[2 + tt * 128: 2 + (tt + 1) * 128, csl], True, False)
        prog_barrier(p)

        Rgain = Res(); Rx32 = [Res(), Res()]; Rjunk = Res(); Rxs = Res(); Rst = [Res() for _ in range(4)]
        Rxn = [[Res() for _ in range(4)] for _ in range(NTT)]; Rxnh = Res()
        p.dma(SP, lambda q: q.dma_start(out=gainb, in_=gconv), writes=[Rgain])
        nbk = (FC + 7) // 8
        for t in range(NTT + 1):
            xb = t % 2; sj = t % 4
            bank_ids = [((t % 2) * 4 + i) % 8 for i in range(nbk)]
            if t < NTT:
                _prep_tile(p, nc, h1s[2 + t * 128: 2 + (t + 1) * 128, :], 128, D, x32[xb], Rx32[xb], junk, Rjunk,
                           small[:, 2 * sj:2 * sj + 1], small[:, 2 * sj + 1:2 * sj + 2], Rst[sj], gainb, Rgain,
                           xs1, Rxs, ident, Rc, banks_bf, RB, bank_ids,
                           lambda c0, n, t=t: xn2T[:, c0:c0 + n, t * 128:(t + 1) * 128], Rxn[t], t)
            else:
                _prep_tile(p, nc, h1s[0:2, :], 2, D, x32[xb], Rx32[xb], junk, Rjunk,
                           small[:, 2 * sj:2 * sj + 1], small[:, 2 * sj + 1:2 * sj + 2], Rst[sj], gainb, Rgain,
                           xs1, Rxs, ident, Rc, banks_bf, RB, bank_ids,
                           lambda c0, n: xn2Th[:, c0:c0 + n, :], Rxnh, t)
        prog_barrier(p)

        Rcw = Res(); Rcuh = Res(); Ruh = Res()
        RyT = [Res() for _ in range(NG)]
        p.dma(SP, lambda q: q.dma_start(out=convw[:, :], in_=convw_d), writes=[Rcw])
        gcnt = 0
        for cb in range(FC):
            wsl = []
            for part in range(4):
                s = wcnt[0] % NW
                wcnt[0] += 1
                nel = FC * 128
                p.dma(SP, lambda q, s=s, cb=cb, part=part, nel=nel: q.dma_start(out=Wfl[s][:, 0:nel], in_=wc[cb, part]), writes=[RWr[s]])
                wsl.append(s)
            hb = banks[7][:, 0:4].rearrange("p (a b) -> p a b", a=2)
            for pi, part in enumerate((1, 2)):
                s = wsl[part]
                for fc in range(FC):
                    p.add(PE, lambda q, pi=pi, s=s, fc=fc: q.matmul(hb[:, pi, :], lhsT=W128[s][:, fc, :], rhs=xn2Th[:, fc, :], start=(fc == 0), stop=(fc == FC - 1)),
                          reads=[RWr[s], Rxnh], writes=[RB[7]])
            p.add(ACT, lambda q: q.copy(out=uh[:, :], in_=hb[:, 1, :]), reads=[RB[7]], writes=[Ruh])
            p.add(DVE, lambda q, cb=cb: q.tensor_tensor(out=cuh[:, cb, :], in0=hb[:, 0, :], in1=uh[:, :], op=ALU.mult), reads=[RB[7], Ruh], writes=[Rcuh])
            for g in range(NG):
                gsl = slice(g * 512, (g + 1) * 512)
                pb = []
                for part in range(4):
                    b = next_bank(7)
                    s = wsl[part]
                    for fc in range(FC):
                        p.add(PE, lambda q, b=b, s=s, fc=fc, gsl=gsl: q.matmul(banks[b][:, :], lhsT=W128[s][:, fc, :], rhs=xn2T[:, fc, gsl], start=(fc == 0), stop=(fc == FC - 1)),
                              reads=[RWr[s]] + [r for i in range(4) for r in Rxn[g * 4 + i]], writes=[RB[b]])
                    pb.append(b)
                r = gcnt % 2
                rp = (gcnt - 1) % 2
                gcnt += 1
                bB, bC, bu, bg = pb
                p.add(ACT, lambda q, r=r, bu=bu: q.copy(out=u_sb[r], in_=banks[bu][:, :]), reads=[RB[bu]], writes=[Ru[r]])
                p.add(DVE, lambda q, r=r, bC=bC: q.tensor_tensor(out=cu_ext[r][:, 2:514], in0=banks[bC][:, :], in1=u_sb[r], op=ALU.mult),
                      reads=[RB[bC], Ru[r]], writes=[Rcu[r]])
                if g == 0:
                    p.add(DVE, lambda q, r=r, cb=cb: q.tensor_copy(out=cu_ext[r][:, 0:2], in_=cuh[:, cb, :]), reads=[Rcuh], writes=[Rcu[r]])
                else:
                    p.add(DVE, lambda q, r=r, rp=rp: q.tensor_copy(out=cu_ext[r][:, 0:2], in_=cu_ext[rp][:, 512:514]), reads=[Rcu[rp]], writes=[Rcu[r]])
                p.add(DVE, lambda q, r=r, cb=cb: q.tensor_scalar(out=tcv[r], in0=cu_ext[r][:, 0:512], scalar1=convw[:, cb * 3:cb * 3 + 1], scalar2=None, op0=ALU.mult),
                      reads=[Rcu[r], Rcw], writes=[Rt[r]])
                for i in (1, 2):
                    p.add(DVE, lambda q, r=r, cb=cb, i=i: q.scalar_tensor_tensor(out=tcv[r], in0=cu_ext[r][:, i:i + 512], scalar=convw[:, cb * 3 + i:cb * 3 + i + 1],
                                                                              in1=tcv[r], op0=ALU.mult, op1=ALU.add),
                          reads=[Rcu[r], Rcw, Rt[r]], writes=[Rt[r]])
                p.add(ACT, lambda q, r=r, bg=bg: q.activation(out=sgb[r], in_=banks[bg][:, :], func=AF.Silu), reads=[RB[bg]], writes=[Rsg[r]])
                p.add(DVE, lambda q, r=r, bB=bB: q.tensor_tensor(out=t2b[r], in0=banks[bB][:, :], in1=tcv[r], op=ALU.mult), reads=[RB[bB], Rt[r]], writes=[Rt2[r]])
                p.add(POOL, lambda q, r=r, cb=cb, gsl=gsl: q.tensor_tensor(out=yT[:, cb, gsl], in0=t2b[r], in1=sgb[r], op=ALU.mult),
                      reads=[Rt2[r], Rsg[r]], writes=[RyT[g]])
        prog_barrier(p)

        RyTall = Res()
        junk2 = ar.view(r3, [512], F32)
        proj_pass(wco, lambda fc, tt: yT[:, fc, tt * 128:(tt + 1) * 128], RyTall,
                  lambda g, csl: h1s[2 + g * 512: 2 + (g + 1) * 512, csl].rearrange("(t p) c -> p t c", p=128),
                  lambda tt, csl: out[tt * 128:(tt + 1) * 128, csl], False, True)
        prog_barrier(p)

        Rgf = Res(); Rh2 = [Res(), Res()]
        p.dma(SP, lambda q: q.dma_start(out=gfinb, in_=gfin), writes=[Rgf])
        outs = []
        for t in range(NTT):
            hb_ = t % 2; sj = t % 4
            ssc = small[:, 2 * sj:2 * sj + 1]; rsc = small[:, 2 * sj + 1:2 * sj + 2]
            p.dma(SP, lambda q, t=t, hb_=hb_: q.dma_start(out=h2b[hb_], in_=out[t * 128:(t + 1) * 128, :]), writes=[Rh2[hb_]])
            p.add(DVE, lambda q, t=t, ssc=ssc: q.reduce_sum(out=ssc, in_=ss2[:, t * CG:(t + 1) * CG], axis=AX), writes=[Rst[sj]])
            p.add(ACT, lambda q, ssc=ssc, rsc=rsc: q.activation(out=rsc, in_=ssc, func=AF.Ln, scale=1.0 / D, bias=RMS_EPS), reads=[Rst[sj]], writes=[Rst[sj]])
            p.add(ACT, lambda q, rsc=rsc: q.activation(out=rsc, in_=rsc, func=AF.Exp, scale=-0.5), reads=[Rst[sj]], writes=[Rst[sj]])
            p.add(DVE, lambda q, hb_=hb_, rsc=rsc: q.scalar_tensor_tensor(out=h2b[hb_], in0=h2b[hb_], scalar=rsc, in1=gfinb, op0=ALU.mult, op1=ALU.mult),
                  reads=[Rh2[hb_], Rst[sj], Rgf], writes=[Rh2[hb_]])
            outs.append(p.dma(POOL, lambda q, t=t, hb_=hb_: q.dma_start(out=out[t * 128:(t + 1) * 128, :], in_=h2b[hb_]), reads=[Rh2[hb_]]))
        p.emit(final_wait_ops=outs[-6:])
    return nc


D_MODEL = 4096
SEQ = 8192
N_CORES = 8
FC_ = D_MODEL // 128
NH_CORE = 4
NT_CORE = SEQ // N_CORES

_NC_CACHE = {}


def _get_nc(name, fn):
    if name not in _NC_CACHE:
        _NC_CACHE[name] = fn()
    return _NC_CACHE[name]


def _lay512(W, FC):
    D = FC * 128
    CG = D // 512
    return np.ascontiguousarray(W.reshape(FC, 128, CG, 512).transpose(2, 1, 0, 3)).reshape(CG, 128, FC * 512)


def kernel(x, norm_attn, w_in_attn, w_out_attn, norm_conv, w_in_conv, conv_w, w_out_conv, final_norm):
    import ml_dtypes
    f32 = np.float32
    D, S, FC = D_MODEL, SEQ, FC_
    x2 = np.ascontiguousarray(np.asarray(x, dtype=f32).reshape(S, D))
    bc = lambda g: np.ascontiguousarray(np.broadcast_to(np.asarray(g, dtype=f32)[None, :], (128, D)))
    W4 = np.asarray(w_in_attn, dtype=f32).reshape(FC, 128, 4, 32, 128)
    gain_a = bc(norm_attn)
    wo = _lay512(np.asarray(w_out_attn, dtype=f32), FC)
    wco = _lay512(np.asarray(w_out_conv, dtype=f32), FC)
    wc = np.ascontiguousarray(np.asarray(w_in_conv, dtype=f32).reshape(FC, 128, 4, FC, 128).transpose(3, 2, 1, 0, 4)).reshape(FC, 4, 128, FC * 128)
    wflat = np.concatenate([wo.reshape(-1), wc.reshape(-1), wco.reshape(-1)])
    n_w = (wo.size, wc.size, wco.size)
    del wo, wc, wco
    per = wflat.size // N_CORES
    NCV = per // (128 * 2048)
    assert NCV * 128 * 2048 * N_CORES == wflat.size
    in_maps = []
    for c in range(N_CORES):
        wd = np.ascontiguousarray(W4[:, :, :, c * NH_CORE:(c + 1) * NH_CORE, :].transpose(3, 1, 0, 2, 4)).reshape(NH_CORE, 128, FC * 512)
        in_maps.append({"x": x2, "gain_b": gain_a, "w": wd, "wcv_in": wflat[c * per:(c + 1) * per].reshape(NCV * 128, 2048)})
    ncA = _get_nc("A", lambda: build_A(S, FC, NH_CORE, NCV))
    resA = run_bass_kernel_spmd(ncA, in_maps, core_ids=list(range(N_CORES)))
    wbf = np.concatenate([np.asarray(resA.results[c]["wcv_out"]).reshape(-1) for c in range(N_CORES)])
    wo = wbf[0:n_w[0]].reshape(FC * 128 // 512, 128, FC * 512)
    wc = wbf[n_w[0]:n_w[0] + n_w[1]].reshape(FC, 4, 128, FC * 128)
    wco = wbf[n_w[0] + n_w[1]:].reshape(FC * 128 // 512, 128, FC * 512)
    del wflat
    ogT = np.concatenate([np.asarray(resA.results[c]["og"]).reshape(NH_CORE * 128, S) for c in range(N_CORES)], axis=0)
    NT = NT_CORE
    cw = np.ascontiguousarray(np.asarray(conv_w, dtype=f32).reshape(3, FC, 128).transpose(2, 1, 0)).reshape(128, FC * 3)
    gconv = bc(norm_conv)
    gfin = bc(final_norm)
    in_maps = []
    for c in range(N_CORES):
        t0 = c * NT
        xh = np.zeros((NT + 2, D), dtype=f32)
        og_c = np.zeros((D, NT + 2), dtype=ogT.dtype)
        if c == 0:
            xh[2:] = x2[0:NT]
            og_c[:, 2:] = ogT[:, 0:NT]
        else:
            xh[:] = x2[t0 - 2:t0 + NT]
            og_c[:] = ogT[:, t0 - 2:t0 + NT]
        og_l = np.ascontiguousarray(og_c.reshape(FC, 128, NT + 2).transpose(1, 0, 2)).reshape(128, FC * (NT + 2))
        in_maps.append({"xh": xh, "ogT": og_l, "wo": wo, "gconv_b": gconv, "wc": wc, "convw": cw, "wco": wco, "gfin_b": gfin})
    ncB = _get_nc("B", lambda: build_B(NT, FC))
    resB = run_bass_kernel_spmd(ncB, in_maps, core_ids=list(range(N_CORES)))
    out = np.concatenate([np.asarray(resB.results[c]["out"], dtype=f32) for c in range(N_CORES)], axis=0)
    return out.reshape(1, S, D)
```

```python
import numpy as np
import concourse.bass as bass
import concourse.mybir as mybir
from concourse.bass_utils import run_bass_kernel_spmd

F32 = mybir.dt.float32
BF16 = mybir.dt.bfloat16
AF = mybir.ActivationFunctionType
ALU = mybir.AluOpType

PE, ACT, DVE, POOL, SP = "tensor", "scalar", "vector", "gpsimd", "sync"
ENGS = (PE, ACT, DVE, POOL, SP)


class Res:
    __slots__ = ("writer", "readers", "name")

    def __init__(self, name=""):
        self.writer = None
        self.readers = {}
        self.name = name


class Op:
    __slots__ = ("eng", "fn", "deps", "is_dma", "sem", "val", "needed", "idx")

    def __init__(self, eng, fn, deps, is_dma):
        self.eng = eng
        self.fn = fn
        self.deps = deps
        self.is_dma = is_dma
        self.sem = None
        self.val = None
        self.needed = False
        self.idx = None


class Prog:
    def __init__(self, nc, same_engine_sync=True, dma_ring=6):
        self.nc = nc
        self.ops = {e: [] for e in ENGS}
        self.same_engine_sync = same_engine_sync
        self.dma_ring = dma_ring
        self.dma_hist = {e: [] for e in ENGS}
        self.n_ops = 0
        self.barrier_deps = []
        self.barrier_seen = set()

    def _deps(self, reads, writes, extra):
        deps = []
        for r in reads:
            if r.writer is not None:
                deps.append(r.writer)
        for w in writes:
            if w.writer is not None:
                deps.append(w.writer)
            deps.extend(w.readers.values())
        deps.extend(extra)
        return deps

    def _bar(self, eng, deps):
        if self.barrier_deps and eng not in self.barrier_seen:
            self.barrier_seen.add(eng)
            deps.extend(self.barrier_deps)
        return deps

    def _commit(self, op, reads, writes):
        for r in reads:
            if op.is_dma:
                r.readers[("dma", id(op))] = op
            else:
                r.readers[op.eng] = op
        for w in writes:
            w.writer = op
            w.readers = {}

    def add(self, eng, fn, reads=(), writes=(), extra=()):
        op = Op(eng, fn, self._bar(eng, self._deps(reads, writes, extra)), False)
        self._commit(op, reads, writes)
        self.ops[eng].append(op)
        self.n_ops += 1
        return op

    def dma(self, eng, fn, reads=(), writes=(), extra=()):
        deps = self._bar(eng, self._deps(reads, writes, extra))
        hist = self.dma_hist[eng]
        if len(hist) >= self.dma_ring:
            deps.append(hist[-self.dma_ring])
        op = Op(eng, fn, deps, True)
        op.idx = len(hist)
        hist.append(op)
        self._commit(op, reads, writes)
        self.ops[eng].append(op)
        self.n_ops += 1
        return op

    def emit(self, final_wait_ops=()):
        nc = self.nc
        for e in ENGS:
            for op in self.ops[e]:
                for d in op.deps:
                    if d.is_dma or d.eng != op.eng or (self.same_engine_sync and op.eng != PE) or op.is_dma:
                        d.needed = True
        for op in final_wait_ops:
            op.needed = True
        import contextlib
        with contextlib.ExitStack() as st:
            eng_sem = {e: st.enter_context(nc.semaphore("p_" + e)) for e in (PE, ACT, DVE, POOL)}
            dma_sems = {e: [st.enter_context(nc.semaphore("d_%s_%d" % (e, i))) for i in range(self.dma_ring)]
                        for e in ENGS if self.dma_hist[e]}
            for e in ENGS:
                cnt = 0
                ring_cnt = [0] * self.dma_ring
                for op in self.ops[e]:
                    if op.is_dma:
                        s = op.idx % self.dma_ring
                        ring_cnt[s] += 16
                        op.sem = dma_sems[e][s]
                        op.val = ring_cnt[s]
                    elif op.needed:
                        cnt += 1
                        op.sem = eng_sem[e]
                        op.val = cnt
            block = st.enter_context(nc.Block())
            for e in ENGS:
                ops = self.ops[e]
                if not ops and not (e == SP and final_wait_ops):
                    continue
                tail = list(final_wait_ops) if e == SP else []

                def body(eng, ops=ops, e=e, tail=tail):
                    waited = {}
                    def do_wait(d):
                        if d.sem is None:
                            return
                        k = id(d.sem)
                        if waited.get(k, 0) >= d.val:
                            return
                        eng.wait_ge(d.sem, d.val)
                        waited[k] = d.val
                    for op in ops:
                        best = {}
                        for d in op.deps:
                            if (not d.is_dma) and d.eng == e and not op.is_dma:
                                if e == PE or not self.same_engine_sync:
                                    continue
                            if d.sem is None:
                                continue
                            k = id(d.sem)
                            if k not in best or best[k].val < d.val:
                                best[k] = d
                        for d in best.values():
                            do_wait(d)
                        ins = op.fn(eng)
                        if op.is_dma:
                            ins.then_inc(op.sem, 16)
                        elif op.needed:
                            ins.then_inc(op.sem, 1)
                    for d in tail:
                        do_wait(d)
                getattr(block, e)(body)


def prog_barrier(p):
    deps = []
    for e in ENGS:
        ops = p.ops[e]
        comp = [o for o in ops if not o.is_dma]
        if comp:
            deps.append(comp[-1])
        deps.extend(p.dma_hist[e][-p.dma_ring:])
    p.barrier_deps = deps
    p.barrier_seen = set()


class Arena:
    def __init__(self, t, nbytes):
        self.t = t
        self.n = nbytes
        self.top = 0

    def alloc(self, nbytes, align=64):
        off = (self.top + align - 1) // align * align
        assert off + nbytes <= self.n, ("arena overflow", off, nbytes, self.n)
        self.top = off + nbytes
        return off

    def view(self, off, shape, dt, parts=128):
        esz = 2 if dt == BF16 else 4
        n = int(np.prod(shape))
        ap = self.t[0:parts, off // 2: off // 2 + n * esz // 2]
        if dt != BF16:
            ap = ap.bitcast(dt)
        if len(shape) == 2:
            ap = ap.rearrange("p (a b) -> p a b", a=shape[0])
        elif len(shape) == 3:
            ap = ap.rearrange("p (a b c) -> p a b c", a=shape[0], b=shape[1])
        return ap


import contextlib

RMS_EPS = 1e-6
KIB = 1024


def _consts(nc, st, p):
    ident = st.enter_context(nc.sbuf_tensor("ident", [128, 128], BF16))
    U = st.enter_context(nc.sbuf_tensor("Utri", [128, 128], BF16))
    NU = st.enter_context(nc.sbuf_tensor("NUtri", [128, 128], BF16))
    Rc = Res("consts")
    p.add(POOL, lambda q: q.memset(ident[:], 1.0), writes=[Rc])
    p.add(POOL, lambda q: q.affine_select(out=ident[:], in_=ident[:], pattern=[[-1, 128]], compare_op=ALU.is_equal,
                                          fill=0.0, base=0, channel_multiplier=1), writes=[Rc])
    p.add(POOL, lambda q: q.memset(U[:], 1.0), writes=[Rc])
    p.add(POOL, lambda q: q.affine_select(out=U[:], in_=U[:], pattern=[[-1, 128]], compare_op=ALU.is_ge,
                                          fill=0.0, base=0, channel_multiplier=1), writes=[Rc])
    p.add(POOL, lambda q: q.memset(NU[:], 1.0), writes=[Rc])
    p.add(POOL, lambda q: q.affine_select(out=NU[:], in_=NU[:], pattern=[[1, 128]], compare_op=ALU.is_gt,
                                          fill=0.0, base=0, channel_multiplier=-1), writes=[Rc])
    return ident, U, NU, Rc


def _prep_tile(p, nc, src_ap, npart, D, x32, Rx32, junk, Rjunk, ss, rs, Rst, gainb, Rgain, xs, Rxs,
               ident, Rc, banks_bf, RB, bank_ids, dst_fn, Rdst, ev_flip, phase=0):
    if phase in (0, 1):
        _prep_tile_a(p, nc, src_ap, npart, D, x32, Rx32, junk, Rjunk, ss, rs, Rst, gainb, Rgain, xs, Rxs)
    if phase in (0, 2):
        _prep_tile_b(p, npart, D, xs, Rxs, ident, Rc, banks_bf, RB, bank_ids, dst_fn, Rdst, ev_flip)


def _prep_tile_a(p, nc, src_ap, npart, D, x32, Rx32, junk, Rjunk, ss, rs, Rst, gainb, Rgain, xs, Rxs):
    FC = D // 128
    p.dma(SP, lambda q: q.dma_start(out=x32[0:npart, :], in_=src_ap), writes=[Rx32])
    p.add(ACT, lambda q: q.activation(out=junk[0:npart, :], in_=x32[0:npart, :], func=AF.Square, accum_out=ss[0:npart, :]),
          reads=[Rx32], writes=[Rjunk, Rst])
    p.add(ACT, lambda q: q.activation(out=rs[0:npart, :], in_=ss[0:npart, :], func=AF.Ln, scale=1.0 / D, bias=RMS_EPS),
          reads=[Rst], writes=[Rst])
    p.add(ACT, lambda q: q.activation(out=rs[0:npart, :], in_=rs[0:npart, :], func=AF.Exp, scale=-0.5), reads=[Rst], writes=[Rst])
    p.add(DVE, lambda q: q.scalar_tensor_tensor(out=xs[0:npart, :], in0=x32[0:npart, :], scalar=rs[0:npart, 0:1],
                                                in1=gainb[0:npart, :], op0=ALU.mult, op1=ALU.mult),
          reads=[Rx32, Rst, Rgain], writes=[Rxs])


def _prep_tile_b(p, npart, D, xs, Rxs, ident, Rc, banks_bf, RB, bank_ids, dst_fn, Rdst, ev_flip):
    FC = D // 128
    nb = (FC + 7) // 8
    for bi in range(nb):
        b = bank_ids[bi]
        c0 = bi * 8
        n = min(8, FC - c0)
        for c in range(c0, c0 + n):
            p.add(PE, lambda q, c=c, b=b, c0=c0: q.transpose(out=banks_bf[b][:, c - c0, 0:npart], in_=xs[0:npart, c * 128:(c + 1) * 128],
                                                         identity=ident[0:npart, 0:npart]),
                  reads=[Rxs, Rc], writes=[RB[b]])
        eng = ACT if (bi + ev_flip) % 2 == 0 else DVE
        rd = Rdst[bi] if isinstance(Rdst, (list, tuple)) else Rdst
        if eng == ACT:
            p.add(ACT, lambda q, b=b, c0=c0, n=n: q.copy(out=dst_fn(c0, n), in_=banks_bf[b][:, 0:n, 0:npart]),
                  reads=[RB[b]], writes=[rd])
        else:
            p.add(DVE, lambda q, b=b, c0=c0, n=n: q.tensor_copy(out=dst_fn(c0, n), in_=banks_bf[b][:, 0:n, 0:npart]),
                  reads=[RB[b]], writes=[rd])


def build_A(S, FC, NH, NCV=0):
    D = FC * 128
    NTG = S // 512
    NKB = S // 128
    scale = 1.0 / (128 ** 0.5)
    nc = bass.Bass("TRN2", target_bir_lowering=False)
    x = nc.dram_tensor("x", [S, D], F32, kind="ExternalInput").ap()
    gain_d = nc.dram_tensor("gain_b", [128, D], F32, kind="ExternalInput").ap()
    w = nc.dram_tensor("w", [NH, 128, FC * 512], F32, kind="ExternalInput").ap()
    og = nc.dram_tensor("og", [NH, 128, S], BF16, kind="ExternalOutput").ap()
    scr = nc.dram_tensor("xT_scr", [NTG, 128, FC * 512], BF16, kind="Internal").ap()
    if NCV:
        wcv_in = nc.dram_tensor("wcv_in", [NCV * 128, 2048], F32, kind="ExternalInput").ap()
        wcv_out = nc.dram_tensor("wcv_out", [NCV * 128, 2048], BF16, kind="ExternalOutput").ap()
    cv_next = [0]

    with contextlib.ExitStack() as st:
        p = Prog(nc)
        ident, U, NU, Rc = _consts(nc, st, p)
        small = st.enter_context(nc.sbuf_tensor("small", [128, 16], F32))
        NX32 = 4
        szP = D * 4 + NX32 * D * 4 + 2 * D * 2 + 2 * FC * 1024
        szH = 4 * S * 2 + FC * 1024 + 2 * FC * 1024
        ring_sz = 5 * 2048 + 4 * 1024 + 3 * 2048 + 3 * 1024 + 2 * 1024 + 2 * 2048 + 2 * 1024
        tot = max(szP, szH) + ring_sz + 1024
        art = st.enter_context(nc.sbuf_tensor("arena", [128, tot // 2], BF16))
        ar = Arena(art, tot)
        base = ar.alloc(max(szP, szH))
        o = base
        gainb = ar.view(o, [D], F32); o += D * 4
        x32 = []
        for i in range(NX32):
            x32.append(ar.view(o, [D], F32)); o += D * 4
        xs = []
        for i in range(2):
            xs.append(ar.view(o, [D], BF16)); o += D * 2
        xTg, xTg_flat = [], []
        for i in range(2):
            xTg.append(ar.view(o, [FC, 512], BF16)); xTg_flat.append(ar.view(o, [FC * 512], BF16)); o += FC * 1024
        o = base
        qT = ar.view(o, [S], BF16); o += S * 2
        sgT = ar.view(o, [S], BF16); o += S * 2
        kT = ar.view(o, [S], BF16); o += S * 2
        vsb = ar.view(o, [NKB, 128], BF16); o += S * 2
        Wh = ar.view(o, [FC, 512], BF16); Wh_flat = ar.view(o, [FC * 512], BF16); o += FC * 1024
        xin, xin_flat = [], []
        for i in range(2):
            xin.append(ar.view(o, [FC, 512], BF16)); xin_flat.append(ar.view(o, [FC * 512], BF16)); o += FC * 1024
        def ring(n, dt):
            esz = 4 if dt == F32 else 2
            return [ar.view(ar.alloc(512 * esz), [512], dt) for _ in range(n)]
        e_r = ring(5, F32); L_r = ring(4, BF16); w_r = ring(3, F32); a_r = ring(3, BF16)
        vst = ring(2, BF16); gtm = ring(2, F32); ogs = ring(2, BF16)
        Re = [Res() for _ in e_r]; RL = [Res() for _ in L_r]; Rw = [Res() for _ in w_r]; Ra = [Res() for _ in a_r]
        Rvst = [Res() for _ in vst]; Rgtm = [Res() for _ in gtm]; Rogs = [Res() for _ in ogs]
        banks = [st.enter_context(nc.psum_tensor("bank%d" % i, [128, 512], F32)) for i in range(8)]
        banks_bf = [b.bitcast(BF16)[:, :].rearrange("p (a b) -> p a b", a=8) for b in banks]
        RB = [Res("bank%d" % i) for i in range(8)]

        Rgain = Res(); Rx32 = [Res() for _ in range(NX32)]; Rxs = [Res(), Res()]
        Rst = [Res() for _ in range(4)]
        RxTg = [[[Res() for _ in range(4)] for _ in range(4)] for _ in range(2)]
        Rscr = [Res() for _ in range(NTG)]
        p.dma(SP, lambda q: q.dma_start(out=gainb, in_=gain_d), writes=[Rgain])
        nbk = (FC + 7) // 8
        def stageP(t, phase):
            tg, tt = t // 4, t % 4
            bsel = tg % 2
            xb = t % 2
            x4 = t % NX32
            sj = t % 4
            bank_ids = [((t % 2) * 4 + i) % 8 for i in range(nbk)]
            _prep_tile(p, nc, x[t * 128:(t + 1) * 128, :], 128, D, x32[x4], Rx32[x4], xs[xb], Rxs[xb],
                       small[:, 2 * sj:2 * sj + 1], small[:, 2 * sj + 1:2 * sj + 2], Rst[sj], gainb, Rgain,
                       xs[xb], Rxs[xb], ident, Rc, banks_bf, RB, bank_ids,
                       lambda c0, n, bsel=bsel, tt=tt: xTg[bsel][:, c0:c0 + n, tt * 128:(tt + 1) * 128],
                       RxTg[bsel][tt], t, phase=phase)
            if phase == 2 and tt == 3:
                p.dma(POOL, lambda q, tg=tg, bsel=bsel: q.dma_start(out=scr[tg], in_=xTg_flat[bsel]),
                      reads=[r for rr in RxTg[bsel] for r in rr], writes=[Rscr[tg]])
        NT_ = NTG * 4
        stageP(0, 1)
        for t in range(NT_):
            if t + 1 < NT_:
                stageP(t + 1, 1)
            stageP(t, 2)
        prog_barrier(p)

        RW = Res(); Rxin = [Res(), Res()]
        RqT = [Res() for _ in range(NTG)]; RsgT = [Res() for _ in range(NTG)]
        RkT = [Res() for _ in range(NTG)]; Rv = [Res() for _ in range(NTG)]
        out_ops = []
        bank_rr = [0]
        WCH = 2048
        IPB = [5, 6]
        TB = 7
        ZB = [0, 1]; CB = 2; OB = [3, 4]
        CHUNK = 2
        NCH = 4 * (FC // CHUNK) + 1

        def load_W(h):
            for k in range(FC * 512 // WCH):
                p.dma(POOL, lambda q, h=h, k=k: q.dma_start(out=Wh_flat[:, k * WCH:(k + 1) * WCH], in_=w[h][:, k * WCH:(k + 1) * WCH]),
                      writes=[RW])

        def load_xin(tg):
            xb = tg % 2
            p.dma(SP, lambda q, tg=tg, xb=xb: q.dma_start(out=xin_flat[xb], in_=scr[tg]), reads=[Rscr[tg]], writes=[Rxin[xb]])

        def unit_gen(h, tg):
            xb = tg % 2
            tsl = slice(tg * 512, (tg + 1) * 512)
            for j in (1, 2, 0, 3):
                b = IPB[bank_rr[0] % 2]
                bank_rr[0] += 1
                for c in range(FC):
                    p.add(PE, lambda q, b=b, c=c, j=j, xb=xb: q.matmul(banks[b][:, :], lhsT=Wh[:, c, j * 128:(j + 1) * 128], rhs=xin[xb][:, c, :],
                                                                     start=(c == 0), stop=(c == FC - 1)),
                          reads=[RW, Rxin[xb]], writes=[RB[b]])
                    if c % CHUNK == CHUNK - 1 and c != FC - 1:
                        yield 1
                if j == 0:
                    p.add(DVE, lambda q, b=b, tsl=tsl: q.tensor_copy(out=qT[:, tsl], in_=banks[b][:, :]), reads=[RB[b]], writes=[RqT[tg]])
                elif j == 1:
                    p.add(DVE, lambda q, b=b, tsl=tsl: q.tensor_copy(out=kT[:, tsl], in_=banks[b][:, :]), reads=[RB[b]], writes=[RkT[tg]])
                elif j == 2:
                    r = tg % 2
                    p.add(DVE, lambda q, b=b, r=r: q.tensor_copy(out=vst[r], in_=banks[b][:, :]), reads=[RB[b]], writes=[Rvst[r]])
                    for i in range(4):
                        p.add(PE, lambda q, r=r, i=i: q.transpose(out=banks_bf[TB][:, i, :], in_=vst[r][:, i * 128:(i + 1) * 128], identity=ident[:, :]),
                              reads=[Rvst[r], Rc], writes=[RB[TB]])
                    p.add(ACT, lambda q, tg=tg: q.copy(out=vsb[:, tg * 4:(tg + 1) * 4, :], in_=banks_bf[TB][:, 0:4, :]),
                          reads=[RB[TB]], writes=[Rv[tg]])
                else:
                    r = tg % 2
                    p.add(ACT, lambda q, b=b, r=r: q.activation(out=gtm[r], in_=banks[b][:, :], func=AF.Exp, scale=-1.0),
                          reads=[RB[b]], writes=[Rgtm[r]])
                    p.add(DVE, lambda q, r=r: q.tensor_scalar(out=gtm[r], in0=gtm[r], scalar1=1.0, scalar2=None, op0=ALU.add),
                          reads=[Rgtm[r]], writes=[Rgtm[r]])
                    p.add(DVE, lambda q, r=r: q.reciprocal(out=gtm[r], in_=gtm[r]), reads=[Rgtm[r]], writes=[Rgtm[r]])
                    p.add(DVE, lambda q, b=b, r=r, tsl=tsl: q.tensor_tensor(out=sgT[:, tsl], in0=banks[b][:, :], in1=gtm[r], op=ALU.mult),
                          reads=[RB[b], Rgtm[r]], writes=[RsgT[tg]])
                yield 1
            for _ in range(2):
                if cv_next[0] < NCV:
                    k = cv_next[0]
                    cv_next[0] += 1
                    out_ops.append(p.dma(POOL, lambda q, k=k: q.dma_start(out=wcv_out[k * 128:(k + 1) * 128, :], in_=wcv_in[k * 128:(k + 1) * 128, :])))

        def attention(h):
            st_ = {"done": 0, "gen": None, "left": 0}

            def start_unit():
                tg = st_["done"]
                st_["gen"] = unit_gen(h, tg)
                st_["left"] = NCH

            def advance(n):
                for _ in range(n):
                    if st_["gen"] is None:
                        if st_["done"] >= NTG:
                            return
                        start_unit()
                    if next(st_["gen"], None) is None:
                        st_["gen"] = None
                        st_["done"] += 1
                        nxt = st_["done"] + 1
                        if nxt < NTG:
                            load_xin(nxt)
                        return
                    st_["left"] -= 1

            def pump_until(n_units):
                while st_["done"] < min(n_units, NTG):
                    advance(1)

            load_xin(0)
            if NTG > 1:
                load_xin(1)
            pump_until(1)

            steps = []
            first_step = {}
            for T in range(NTG):
                n = 4 * T + 4
                first_step[T] = len(steps)
                for i in range(n):
                    steps.append((T, i, 4 * T + 3 - i, i == n - 1))
            N = len(steps)

            def valid(j):
                return 0 <= j < N
            pend_av = []
            for j in range(-3, N + 2):
                if valid(j - 1):
                    wr = (j - 1) % 3
                    p.add(ACT, lambda q, wr=wr: q.activation(out=w_r[wr], in_=banks[CB][:, :], func=AF.Exp, scale=-1.0),
                          reads=[RB[CB]], writes=[Rw[wr]])
                if valid(j + 2):
                    T, i, kb, last = steps[j + 2]
                    zb = ZB[(j + 2) % 2]; er = (j + 2) % 5
                    p.add(ACT, lambda q, zb=zb, er=er: q.activation(out=e_r[er], in_=banks[zb][:, :], func=AF.Exp, scale=scale),
                          reads=[RB[zb]], writes=[Re[er]])
                    if i < 4:
                        p.add(POOL, lambda q, er=er, i=i: q.affine_select(out=e_r[er], in_=e_r[er], pattern=[[1, 512]], compare_op=ALU.is_gt,
                                                                      fill=0.0, base=-(3 - i) * 128, channel_multiplier=-1),
                              reads=[Re[er]], writes=[Re[er]])
                if valid(j + 1):
                    er = (j + 1) % 5; lr = (j + 1) % 4
                    p.add(ACT, lambda q, er=er, lr=lr: q.activation(out=L_r[lr], in_=e_r[er], func=AF.Ln, bias=1.0),
                          reads=[Re[er]], writes=[RL[lr]])
                if valid(j):
                    T, i, kb, last = steps[j]
                    if st_["done"] < NTG and st_["done"] <= T + 1:
                        iters_left = max(1, (4 * T + 4) - i - 3)
                        if st_["gen"] is None:
                            start_unit()
                        advance(-(-st_["left"] // iters_left))
                if valid(j):
                    T, i, kb, last = steps[j]
                    lr = j % 4; lp = (j - 1) % 4
                    if i > 0:
                        p.add(PE, lambda q, lp=lp: q.matmul(banks[CB][:, :], lhsT=NU[:, :], rhs=L_r[lp], start=False, stop=False, skip_group_check=True),
                              reads=[Rc, RL[lp]], writes=[RB[CB]])
                    p.add(PE, lambda q, lr=lr, i=i: q.matmul(banks[CB][:, :], lhsT=U[:, :], rhs=L_r[lr], start=(i == 0), stop=True, skip_group_check=(i > 0)),
                          reads=[Rc, RL[lr]], writes=[RB[CB]])
                if valid(j + 3):
                    T, i, kb, last = steps[j + 3]
                    if i == 0:
                        pump_until(T + 1)
                    zb = ZB[(j + 3) % 2]
                    p.add(PE, lambda q, zb=zb, kb=kb, T=T: q.matmul(banks[zb][:, :], lhsT=kT[:, kb * 128:(kb + 1) * 128], rhs=qT[:, T * 512:(T + 1) * 512],
                                                                start=True, stop=True),
                          reads=[RkT[kb // 4], RqT[T]], writes=[RB[zb]])
                if valid(j - 1):
                    T, i, kb, last = steps[j - 1]
                    er = (j - 1) % 5; wr = (j - 1) % 3; ai = (j - 1) % 3
                    ob = OB[T % 2]
                    p.add(DVE, lambda q, er=er, wr=wr, ai=ai: q.tensor_tensor(out=a_r[ai], in0=e_r[er], in1=w_r[wr], op=ALU.mult),
                          reads=[Re[er], Rw[wr]], writes=[Ra[ai]])
                    pend_av.append((ob, kb, ai, i, last))
                    if len(pend_av) == 2 or last or (j - 1) == N - 1:
                        for (ob_, kb_, ai_, i_, last_) in pend_av:
                            p.add(PE, lambda q, ob_=ob_, kb_=kb_, ai_=ai_, i_=i_, last_=last_: q.matmul(banks[ob_][:, :], lhsT=vsb[:, kb_, :], rhs=a_r[ai_], start=(i_ == 0), stop=last_),
                                  reads=[Rv[kb_ // 4], Ra[ai_]], writes=[RB[ob_]])
                        del pend_av[:]
                    if last:
                        sr = T % 2
                        p.add(DVE, lambda q, ob=ob, sr=sr, T=T: q.tensor_tensor(out=ogs[sr], in0=banks[ob][:, :], in1=sgT[:, T * 512:(T + 1) * 512], op=ALU.mult),
                              reads=[RB[ob], RsgT[T]], writes=[Rogs[sr]])
                        out_ops.append(p.dma(POOL, lambda q, h=h, sr=sr, T=T: q.dma_start(out=og[h][:, T * 512:(T + 1) * 512], in_=ogs[sr]),
                                             reads=[Rogs[sr]]))
                    if T == NTG - 1 and i == 0 and h + 1 < NH:
                        load_W(h + 1)
            pump_until(NTG)

        load_W(0)
        for h in range(NH):
            attention(h)

        p.emit(final_wait_ops=out_ops[-6:])
    return nc


def build_B(NT, FC):
    D = FC * 128
    CG = D // 512
    NTT = NT // 128
    NG = NT // 512
    NH4 = NTT // 4
    AX = mybir.AxisListType.X
    nc = bass.Bass("TRN2", target_bir_lowering=False)
    xh = nc.dram_tensor("xh", [NT + 2, D], F32, kind="ExternalInput").ap()
    ogT_d = nc.dram_tensor("ogT", [128, FC * (NT + 2)], BF16, kind="ExternalInput").ap()
    wo = nc.dram_tensor("wo", [CG, 128, FC * 512], BF16, kind="ExternalInput").ap()
    gconv = nc.dram_tensor("gconv_b", [128, D], F32, kind="ExternalInput").ap()
    wc = nc.dram_tensor("wc", [FC, 4, 128, FC * 128], BF16, kind="ExternalInput").ap()
    convw_d = nc.dram_tensor("convw", [128, FC * 3], F32, kind="ExternalInput").ap()
    wco = nc.dram_tensor("wco", [CG, 128, FC * 512], BF16, kind="ExternalInput").ap()
    gfin = nc.dram_tensor("gfin_b", [128, D], F32, kind="ExternalInput").ap()
    out = nc.dram_tensor("out", [NT, D], F32, kind="ExternalOutput").ap()
    h1s = nc.dram_tensor("h1s", [NT + 2, D], F32, kind="Internal").ap()

    with contextlib.ExitStack() as st:
        p = Prog(nc)
        ident, U, NU, Rc = _consts(nc, st, p)
        small = st.enter_context(nc.sbuf_tensor("small", [128, 16], F32))
        convw = st.enter_context(nc.sbuf_tensor("convw_sb", [128, FC * 3], F32))
        cuh = st.enter_context(nc.sbuf_tensor("cuh", [128, FC, 2], F32))
        uh = st.enter_context(nc.sbuf_tensor("uh", [128, 2], F32))
        ss2 = st.enter_context(nc.sbuf_tensor("ss2", [128, NTT * CG], F32))
        xn2Th = st.enter_context(nc.sbuf_tensor("xn2Th", [128, FC, 2], BF16))
        szR1 = max(FC * (NT + 2) * 2, 2 * D * 4 + D * 2 + D * 2 + D * 4, FC * NT * 2)
        szR1 = (szR1 + 63) // 64 * 64
        szR3 = max(FC * NT * 2, 3 * D * 4)
        WCH = 8192
        NW = 5
        szR2 = 3 * 8192
        tot = szR1 + szR3 + NW * WCH + szR2 + 4 * 2048 + 1024
        art = st.enter_context(nc.sbuf_tensor("arena", [128, tot // 2], BF16))
        ar = Arena(art, tot)
        r1 = ar.alloc(szR1); r3 = ar.alloc(szR3); wr = ar.alloc(NW * WCH); r2 = ar.alloc(szR2); sg_off = ar.alloc(4 * 2048)
        ogT = ar.view(r1, [FC, NT + 2], BF16); ogT_flat = ar.view(r1, [FC * (NT + 2)], BF16)
        o = r1
        x32 = []
        for i in range(2):
            x32.append(ar.view(o, [D], F32)); o += D * 4
        junk = ar.view(o, [D], BF16); o += D * 2
        xs1 = ar.view(o, [D], BF16); o += D * 2
        gainb = ar.view(o, [D], F32); o += D * 4
        yT = ar.view(r1, [FC, NT], BF16)
        xn2T = ar.view(r3, [FC, NT], BF16)
        h2b = [ar.view(r3 + i * D * 4, [D], F32) for i in range(2)]
        gfinb = ar.view(r3 + 2 * D * 4, [D], F32)
        Wfl = [ar.view(wr + i * WCH, [WCH // 2], BF16) for i in range(NW)]
        W512 = [ar.view(wr + i * WCH, [8, 512], BF16) for i in range(NW)]
        W128 = [ar.view(wr + i * WCH, [FC, 128], BF16) for i in range(NW)] if FC * 128 * 2 <= WCH else None
        RWr = [Res() for _ in range(NW)]
        wcnt = [0]
        colt = [ar.view(r2 + i * 8192, [4, 512], F32) for i in range(3)]
        Rcolt = [Res() for _ in range(3)]
        gt = [ar.view(r2 + i * 2048, [512], F32) for i in range(10)]
        cu_ext = [ar.view(r2 + 0, [1024], F32), ar.view(r2 + 4096, [1024], F32)]
        u_sb = [gt[4], gt[5]]; tcv = [gt[6], gt[7]]; sgb = [gt[8], gt[9]]; t2b = [gt[10 - 10 + 0 + 0] if False else ar.view(r2 + 20480, [512], F32), ar.view(r2 + 22528, [512], F32)]
        Rcu = [Res(), Res()]; Ru = [Res(), Res()]; Rt = [Res(), Res()]; Rsg = [Res(), Res()]; Rt2 = [Res(), Res()]
        stg = [ar.view(sg_off + i * 2048, [512], F32) for i in range(4)]
        Rstg = [Res() for _ in range(4)]
        scnt = [0]
        banks = [st.enter_context(nc.psum_tensor("bank%d" % i, [128, 512], F32)) for i in range(8)]
        banks_bf = [b.bitcast(BF16)[:, :].rearrange("p (a b) -> p a b", a=8) for b in banks]
        RB = [Res("bank%d" % i) for i in range(8)]
        bcnt = [0]

        def next_bank(n=7):
            b = bcnt[0] % n
            bcnt[0] += 1
            return b

        def load_w512(src_rows):
            s = wcnt[0] % NW
            wcnt[0] += 1
            p.dma(SP, lambda q, s=s: q.dma_start(out=Wfl[s], in_=src_rows), writes=[RWr[s]])
            return s

        RogT = Res()
        nsp = 4
        tot_og = FC * (NT + 2)
        step_og = (tot_og + nsp - 1) // nsp
        for k in range(nsp):
            a0, a1 = k * step_og, min(tot_og, (k + 1) * step_og)
            p.dma(SP, lambda q, a0=a0, a1=a1: q.dma_start(out=ogT_flat[:, a0:a1], in_=ogT_d[:, a0:a1]), writes=[RogT])
        Rh1 = Res()

        def proj_pass(w_dram, lhs_fn, lhs_res, add_src_fn, dst_fn, halo, sumsq):
            colcnt = 0
            for cg in range(CG):
                slots = [load_w512(w_dram[cg][:, k * 4096:(k + 1) * 4096]) for k in range(FC // 8)]
                csl = slice(cg * 512, (cg + 1) * 512)
                cts = []
                for g in range(NH4):
                    ci = colcnt % 3
                    colcnt += 1
                    src = add_src_fn(g, csl)
                    p.dma(SP, lambda q, ci=ci, src=src: q.dma_start(out=colt[ci], in_=src), writes=[Rcolt[ci]])
                    cts.append(ci)
                for tt in range(NTT):
                    b = next_bank(7)
                    for fc in range(FC):
                        s = slots[fc // 8]
                        p.add(PE, lambda q, b=b, fc=fc, s=s, tt=tt: q.matmul(banks[b][:, :], lhsT=lhs_fn(fc, tt), rhs=W512[s][:, fc % 8, :],
                                                                           start=(fc == 0), stop=(fc == FC - 1)),
                              reads=[lhs_res, RWr[s]], writes=[RB[b]])
                    si = scnt[0] % 4
                    scnt[0] += 1
                    ci = cts[tt // 4]
                    p.add(DVE, lambda q, b=b, si=si, ci=ci, tt=tt: q.tensor_tensor(out=stg[si], in0=banks[b][:, :], in1=colt[ci][:, tt % 4, :], op=ALU.add),
                          reads=[RB[b], Rcolt[ci]], writes=[Rstg[si]])
                    if sumsq:
                        col = tt * CG + cg
                        p.add(ACT, lambda q, si=si, col=col: q.activation(out=junk2, in_=stg[si], func=AF.Square, accum_out=ss2[:, col:col + 1]),
                              reads=[Rstg[si]], writes=[Rjunk2])
                    p.dma(POOL, lambda q, si=si, tt=tt, csl=csl: q.dma_start(out=dst_fn(tt, csl), in_=stg[si]), reads=[Rstg[si]], writes=[Rh1])
                if halo:
                    b = 7
                    for fc in range(FC):
                        s = slots[fc // 8]
                        p.add(PE, lambda q, fc=fc, s=s: q.matmul(banks[b][0:2, :], lhsT=ogT[:, fc, 0:2], rhs=W512[s][:, fc % 8, :],
                                                               start=(fc == 0), stop=(fc == FC - 1)),
                              reads=[lhs_res, RWr[s]], writes=[RB[b]])
                    si = scnt[0] % 4
                    scnt[0] += 1
                    ci = colcnt % 3
                    colcnt += 1
                    p.dma(SP, lambda q, ci=ci, csl=csl: q.dma_start(out=colt[ci][0:2, 0, :], in_=xh[0:2, csl]), writes=[Rcolt[ci]])
                    p.add(DVE, lambda q, si=si, ci=ci: q.tensor_tensor(out=stg[si][0:2, :], in0=banks[b][0:2, :], in1=colt[ci][0:2, 0, :], op=ALU.add),
                          reads=[RB[b], Rcolt[ci]], writes=[Rstg[si]])
                    p.dma(POOL, lambda q, si=si, csl=csl: q.dma_start(out=h1s[0:2, csl], in_=stg[si][0:2, :]), reads=[Rstg[si]], writes=[Rh1])

        junk2 = ar.view(sg_off, [512], F32)
        Rjunk2 = Res()
        proj_pass(wo, lambda fc, tt: ogT[:, fc, 2 + tt * 128: 2 + (tt + 1) * 128], RogT,
                  lambda g, csl: xh[2 + g * 512: 2 + (g + 1) * 512, csl].rearrange("(t p) c -> p t c", p=128),
                  lambda tt, csl: h1s[2 + tt * 128: 2 + (tt + 1) * 128, csl], True, False)
        prog_barrier(p)

        Rgain = Res(); Rx32 = [Res(), Res()]; Rjunk = Res(); Rxs = Res(); Rst = [Res() for _ in range(4)]
        Rxn = [[Res() for _ in range(4)] for _ in range(NTT)]; Rxnh = Res()
        p.dma(SP, lambda q: q.dma_start(out=gainb, in_=gconv), writes=[Rgain])
        nbk = (FC + 7) // 8
        for t in range(NTT + 1):
            xb = t % 2; sj = t % 4
            bank_ids = [((t % 2) * 4 + i) % 8 for i in range(nbk)]
            if t < NTT:
                _prep_tile(p, nc, h1s[2 + t * 128: 2 + (t + 1) * 128, :], 128, D, x32[xb], Rx32[xb], junk, Rjunk,
                           small[:, 2 * sj:2 * sj + 1], small[:, 2 * sj + 1:2 * sj + 2], Rst[sj], gainb, Rgain,
                           xs1, Rxs, ident, Rc, banks_bf, RB, bank_ids,
                           lambda c0, n, t=t: xn2T[:, c0:c0 + n, t * 128:(t + 1) * 128], Rxn[t], t)
            else:
                _prep_tile(p, nc, h1s[0:2, :], 2, D, x32[xb], Rx32[xb], junk, Rjunk,
                           small[:, 2 * sj:2 * sj + 1], small[:, 2 * sj + 1:2 * sj + 2], Rst[sj], gainb, Rgain,
                           xs1, Rxs, ident, Rc, banks_bf, RB, bank_ids,
                           lambda c0, n: xn2Th[:, c0:c0 + n, :], Rxnh, t)
        prog_barrier(p)

        Rcw = Res(); Rcuh = Res(); Ruh = Res()
        RyT = [Res() for _ in range(NG)]
        p.dma(SP, lambda q: q.dma_start(out=convw[:, :], in_=convw_d), writes=[Rcw])
        gcnt = 0
        for cb in range(FC):
            wsl = []
            for part in range(4):
                s = wcnt[0] % NW
                wcnt[0] += 1
                nel = FC * 128
                p.dma(SP, lambda q, s=s, cb=cb, part=part, nel=nel: q.dma_start(out=Wfl[s][:, 0:nel], in_=wc[cb, part]), writes=[RWr[s]])
                wsl.append(s)
            hb = banks[7][:, 0:4].rearrange("p (a b) -> p a b", a=2)
            for pi, part in enumerate((1, 2)):
                s = wsl[part]
                for fc in range(FC):
                    p.add(PE, lambda q, pi=pi, s=s, fc=fc: q.matmul(hb[:, pi, :], lhsT=W128[s][:, fc, :], rhs=xn2Th[:, fc, :], start=(fc == 0), stop=(fc == FC - 1)),
                          reads=[RWr[s], Rxnh], writes=[RB[7]])
            p.add(ACT, lambda q: q.copy(out=uh[:, :], in_=hb[:, 1, :]), reads=[RB[7]], writes=[Ruh])
            p.add(DVE, lambda q, cb=cb: q.tensor_tensor(out=cuh[:, cb, :], in0=hb[:, 0, :], in1=uh[:, :], op=ALU.mult), reads=[RB[7], Ruh], writes=[Rcuh])
            for g in range(NG):
                gsl = slice(g * 512, (g + 1) * 512)
                pb = []
                for part in range(4):
                    b = next_bank(7)
                    s = wsl[part]
                    for fc in range(FC):
                        p.add(PE, lambda q, b=b, s=s, fc=fc, gsl=gsl: q.matmul(banks[b][:, :], lhsT=W128[s][:, fc, :], rhs=xn2T[:, fc, gsl], start=(fc == 0), stop=(fc == FC - 1)),
                              reads=[RWr[s]] + [r for i in range(4) for r in Rxn[g * 4 + i]], writes=[RB[b]])
                    pb.append(b)
                r = gcnt % 2
                rp = (gcnt - 1) % 2
                gcnt += 1
                bB, bC, bu, bg = pb
                p.add(ACT, lambda q, r=r, bu=bu: q.copy(out=u_sb[r], in_=banks[bu][:, :]), reads=[RB[bu]], writes=[Ru[r]])
                p.add(DVE, lambda q, r=r, bC=bC: q.tensor_tensor(out=cu_ext[r][:, 2:514], in0=banks[bC][:, :], in1=u_sb[r], op=ALU.mult),
                      reads=[RB[bC], Ru[r]], writes=[Rcu[r]])
                if g == 0:
                    p.add(DVE, lambda q, r=r, cb=cb: q.tensor_copy(out=cu_ext[r][:, 0:2], in_=cuh[:, cb, :]), reads=[Rcuh], writes=[Rcu[r]])
                else:
                    p.add(DVE, lambda q, r=r, rp=rp: q.tensor_copy(out=cu_ext[r][:, 0:2], in_=cu_ext[rp][:, 512:514]), reads=[Rcu[rp]], writes=[Rcu[r]])
                p.add(DVE, lambda q, r=r, cb=cb: q.tensor_scalar(out=tcv[r], in0=cu_ext[r][:, 0:512], scalar1=convw[:, cb * 3:cb * 3 + 1], scalar2=None, op0=ALU.mult),
                      reads=[Rcu[r], Rcw], writes=[Rt[r]])
                for i in (1, 2):
                    p.add(DVE, lambda q, r=r, cb=cb, i=i: q.scalar_tensor_tensor(out=tcv[r], in0=cu_ext[r][:, i:i + 512], scalar=convw[:, cb * 3 + i:cb * 3 + i + 1],
                                                                              in1=tcv[r], op0=ALU.mult, op1=ALU.add),
                          reads=[Rcu[r], Rcw, Rt[r]], writes=[Rt[r]])
                p.add(ACT, lambda q, r=r, bg=bg: q.activation(out=sgb[r], in_=banks[bg][:, :], func=AF.Silu), reads=[RB[bg]], writes=[Rsg[r]])
                p.add(DVE, lambda q, r=r, bB=bB: q.tensor_tensor(out=t2b[r], in0=banks[bB][:, :], in1=tcv[r], op=ALU.mult), reads=[RB[bB], Rt[r]], writes=[Rt2[r]])
                p.add(POOL, lambda q, r=r, cb=cb, gsl=gsl: q.tensor_tensor(out=yT[:, cb, gsl], in0=t2b[r], in1=sgb[r], op=ALU.mult),
                      reads=[Rt2[r], Rsg[r]], writes=[RyT[g]])
        prog_barrier(p)

        RyTall = Res()
        junk2 = ar.view(r3, [512], F32)
        proj_pass(wco, lambda fc, tt: yT[:, fc, tt * 128:(tt + 1) * 128], RyTall,
                  lambda g, csl: h1s[2 + g * 512: 2 + (g + 1) * 512, csl].rearrange("(t p) c -> p t c", p=128),
                  lambda tt, csl: out[tt * 128:(tt + 1) * 128, csl], False, True)
        prog_barrier(p)

        Rgf = Res(); Rh2 = [Res(), Res()]
        p.dma(SP, lambda q: q.dma_start(out=gfinb, in_=gfin), writes=[Rgf])
        outs = []
        for t in range(NTT):
            hb_ = t % 2; sj = t % 4
            ssc = small[:, 2 * sj:2 * sj + 1]; rsc = small[:, 2 * sj + 1:2 * sj + 2]
            p.dma(SP, lambda q, t=t, hb_=hb_: q.dma_start(out=h2b[hb_], in_=out[t * 128:(t + 1) * 128, :]), writes=[Rh2[hb_]])
            p.add(DVE, lambda q, t=t, ssc=ssc: q.reduce_sum(out=ssc, in_=ss2[:, t * CG:(t + 1) * CG], axis=AX), writes=[Rst[sj]])
            p.add(ACT, lambda q, ssc=ssc, rsc=rsc: q.activation(out=rsc, in_=ssc, func=AF.Ln, scale=1.0 / D, bias=RMS_EPS), reads=[Rst[sj]], writes=[Rst[sj]])
            p.add(ACT, lambda q, rsc=rsc: q.activation(out=rsc, in_=rsc, func=AF.Exp, scale=-0.5), reads=[Rst[sj]], writes=[Rst[sj]])
            p.add(DVE, lambda q, hb_=hb_, rsc=rsc: q.scalar_tensor_tensor(out=h2b[hb_], in0=h2b[hb_], scalar=rsc, in1=gfinb, op0=ALU.mult, op1=ALU.mult),
                  reads=[Rh2[hb_], Rst[sj], Rgf], writes=[Rh2[hb_]])
            outs.append(p.dma(POOL, lambda q, t=t, hb_=hb_: q.dma_start(out=out[t * 128:(t + 1) * 128, :], in_=h2b[hb_]), reads=[Rh2[hb_]]))
        p.emit(final_wait_ops=outs[-6:])
    return nc


D_MODEL = 4096
SEQ = 8192
N_CORES = 8
FC_ = D_MODEL // 128
NH_CORE = 4
NT_CORE = SEQ // N_CORES

_NC_CACHE = {}


def _get_nc(name, fn):
    if name not in _NC_CACHE:
        _NC_CACHE[name] = fn()
    return _NC_CACHE[name]


def _lay512(W, FC):
    D = FC * 128
    CG = D // 512
    return np.ascontiguousarray(W.reshape(FC, 128, CG, 512).transpose(2, 1, 0, 3)).reshape(CG, 128, FC * 512)


def kernel(x, norm_attn, w_in_attn, w_out_attn, norm_conv, w_in_conv, conv_w, w_out_conv, final_norm):
    import ml_dtypes
    f32 = np.float32
    D, S, FC = D_MODEL, SEQ, FC_
    x2 = np.ascontiguousarray(np.asarray(x, dtype=f32).reshape(S, D))
    bc = lambda g: np.ascontiguousarray(np.broadcast_to(np.asarray(g, dtype=f32)[None, :], (128, D)))
    W4 = np.asarray(w_in_attn, dtype=f32).reshape(FC, 128, 4, 32, 128)
    gain_a = bc(norm_attn)
    wo = _lay512(np.asarray(w_out_attn, dtype=f32), FC)
    wco = _lay512(np.asarray(w_out_conv, dtype=f32), FC)
    wc = np.ascontiguousarray(np.asarray(w_in_conv, dtype=f32).reshape(FC, 128, 4, FC, 128).transpose(3, 2, 1, 0, 4)).reshape(FC, 4, 128, FC * 128)
    wflat = np.concatenate([wo.reshape(-1), wc.reshape(-1), wco.reshape(-1)])
    n_w = (wo.size, wc.size, wco.size)
    del wo, wc, wco
    per = wflat.size // N_CORES
    NCV = per // (128 * 2048)
    assert NCV * 128 * 2048 * N_CORES == wflat.size
    in_maps = []
    for c in range(N_CORES):
        wd = np.ascontiguousarray(W4[:, :, :, c * NH_CORE:(c + 1) * NH_CORE, :].transpose(3, 1, 0, 2, 4)).reshape(NH_CORE, 128, FC * 512)
        in_maps.append({"x": x2, "gain_b": gain_a, "w": wd, "wcv_in": wflat[c * per:(c + 1) * per].reshape(NCV * 128, 2048)})
    ncA = _get_nc("A", lambda: build_A(S, FC, NH_CORE, NCV))
    resA = run_bass_kernel_spmd(ncA, in_maps, core_ids=list(range(N_CORES)))
    wbf = np.concatenate([np.asarray(resA.results[c]["wcv_out"]).reshape(-1) for c in range(N_CORES)])
    wo = wbf[0:n_w[0]].reshape(FC * 128 // 512, 128, FC * 512)
    wc = wbf[n_w[0]:n_w[0] + n_w[1]].reshape(FC, 4, 128, FC * 128)
    wco = wbf[n_w[0] + n_w[1]:].reshape(FC * 128 // 512, 128, FC * 512)
    del wflat
    ogT = np.concatenate([np.asarray(resA.results[c]["og"]).reshape(NH_CORE * 128, S) for c in range(N_CORES)], axis=0)
    NT = NT_CORE
    cw = np.ascontiguousarray(np.asarray(conv_w, dtype=f32).reshape(3, FC, 128).transpose(2, 1, 0)).reshape(128, FC * 3)
    gconv = bc(norm_conv)
    gfin = bc(final_norm)
    in_maps = []
    for c in range(N_CORES):
        t0 = c * NT
        xh = np.zeros((NT + 2, D), dtype=f32)
        og_c = np.zeros((D, NT + 2), dtype=ogT.dtype)
        if c == 0:
            xh[2:] = x2[0:NT]
            og_c[:, 2:] = ogT[:, 0:NT]
        else:
            xh[:] = x2[t0 - 2:t0 + NT]
            og_c[:] = ogT[:, t0 - 2:t0 + NT]
        og_l = np.ascontiguousarray(og_c.reshape(FC, 128, NT + 2).transpose(1, 0, 2)).reshape(128, FC * (NT + 2))
        in_maps.append({"xh": xh, "ogT": og_l, "wo": wo, "gconv_b": gconv, "wc": wc, "convw": cw, "wco": wco, "gfin_b": gfin})
    ncB = _get_nc("B", lambda: build_B(NT, FC))
    resB = run_bass_kernel_spmd(ncB, in_maps, core_ids=list(range(N_CORES)))
    out = np.concatenate([np.asarray(resB.results[c]["out"], dtype=f32) for c in range(N_CORES)], axis=0)
    return out.reshape(1, S, D)
```
